# Optimizing a Trainium2 kernel written in Bass

```python
import jax, jax.numpy as jnp
from jax import lax
import numpy as np

D_MODEL = 4096
BATCH = 4
SEQ = 2048
DEPTH = 1

HEAD_DIM = 128
ATTN_WIDTH = D_MODEL // 2
CONV_WIDTH = D_MODEL - ATTN_WIDTH
N_Q_HEADS = ATTN_WIDTH // HEAD_DIM
N_KV_HEADS = N_Q_HEADS // 4
Q_PER_KV = N_Q_HEADS // N_KV_HEADS
KV_WIDTH = N_KV_HEADS * HEAD_DIM
CONV_GROUP = 128
N_CONV_GROUPS = CONV_WIDTH // CONV_GROUP
CONV_KERNEL = 31
CONV_PAD = CONV_KERNEL // 2
WINDOW = 128
BLOCK = 128
ROPE_THETA = 500000.0
ROPE_DIM = HEAD_DIM // 4
D_FF = -(-8 * D_MODEL // (3 * 256)) * 256
IN_WIDTH = ATTN_WIDTH + 2 * KV_WIDTH + 2 * CONV_WIDTH
RMS_EPS = 1e-6
LN_EPS = 1e-5

kernel_name = "hymba_conformer_swa_sandwich_layer"


def rms_norm(t, g):
    tf = t.astype(jnp.float32)
    y = tf * lax.rsqrt(jnp.mean(tf * tf, axis=-1, keepdims=True) + RMS_EPS)
    return (y * g.astype(jnp.float32)).astype(t.dtype)


def layer_norm(t, g, b):
    tf = t.astype(jnp.float32)
    mu = jnp.mean(tf, axis=-1, keepdims=True)
    var = jnp.mean(jnp.square(tf - mu), axis=-1, keepdims=True)
    y = (tf - mu) * lax.rsqrt(var + LN_EPS)
    return (y * g.astype(jnp.float32) + b.astype(jnp.float32)).astype(t.dtype)


def rope_tables(positions):
    inv_freq = ROPE_THETA ** (-jnp.arange(0, ROPE_DIM, 2, dtype=jnp.float32) / ROPE_DIM)
    ang = positions.astype(jnp.float32)[..., None] * inv_freq
    return jnp.cos(ang)[:, :, None, :], jnp.sin(ang)[:, :, None, :]


def apply_partial_rope(t, cos, sin):
    half = ROPE_DIM // 2
    tf = t[..., :ROPE_DIM].astype(jnp.float32)
    t1, t2 = tf[..., :half], tf[..., half:]
    rot = jnp.concatenate([t1 * cos - t2 * sin, t2 * cos + t1 * sin], axis=-1)
    return jnp.concatenate([rot.astype(t.dtype), t[..., ROPE_DIM:]], axis=-1)


def windowed_gqa_with_sink(q, k, v, sinks):
    B, S = q.shape[0], q.shape[1]
    nb = S // BLOCK
    span = BLOCK + 2 * WINDOW
    qb = q.reshape(B, nb, BLOCK, N_KV_HEADS, Q_PER_KV, HEAD_DIM)
    pad = ((0, 0), (WINDOW, WINDOW), (0, 0), (0, 0))
    kp = jnp.pad(k, pad)
    vp = jnp.pad(v, pad)
    idx = jnp.arange(nb)[:, None] * BLOCK + jnp.arange(span)[None, :]
    kb = kp[:, idx]
    vb = vp[:, idx]
    s = jnp.einsum('bnqhgd,bnkhd->bnhgqk', qb, kb,
                   preferred_element_type=jnp.float32) * (HEAD_DIM ** -0.5)
    qpos = jnp.arange(nb)[:, None] * BLOCK + jnp.arange(BLOCK)[None, :]
    kpos = idx - WINDOW
    rel = kpos[:, None, :] - qpos[:, :, None]
    valid = (jnp.abs(rel) <= WINDOW) & (kpos[:, None, :] >= 0) & (kpos[:, None, :] < S)
    s = jnp.where(valid[None, :, None, None], s, -jnp.inf)
    sink = sinks.astype(jnp.float32).reshape(1, 1, N_KV_HEADS, Q_PER_KV, 1, 1)
    m = jnp.maximum(jnp.max(s, axis=-1, keepdims=True), sink)
    p = jnp.exp(s - m)
    denom = jnp.sum(p, axis=-1, keepdims=True) + jnp.exp(sink - m)
    p = (p / denom).astype(v.dtype)
    o = jnp.einsum('bnhgqk,bnkhd->bnqhgd', p, vb)
    return o.reshape(B, S, N_Q_HEADS * HEAD_DIM)


def conformer_conv(ca, cg, conv_w, conv_b, ln_g, ln_b):
    u = ca * jax.nn.sigmoid(cg)
    y = lax.conv_general_dilated(
        u, conv_w[:, None, :].astype(u.dtype), window_strides=(1,),
        padding=[(CONV_PAD, CONV_PAD)], dimension_numbers=('NWC', 'WIO', 'NWC'),
        feature_group_count=CONV_WIDTH)
    y = y + conv_b.astype(y.dtype)
    y = layer_norm(y, ln_g, ln_b)
    return jax.nn.silu(y)


def setup_inputs(seed: int = 0) -> dict:
    key = jax.random.key(seed)
    ks = jax.random.split(key, 20)
    f32 = jnp.float32

    def gain(k):
        return 1.0 + 0.02 * jax.random.normal(k, (DEPTH, D_MODEL), f32)

    x = jax.random.normal(ks[0], (BATCH, SEQ, D_MODEL), f32)
    positions = jnp.broadcast_to(jnp.arange(SEQ, dtype=jnp.int32), (BATCH, SEQ))
    return {
        "x": x,
        "positions": positions,
        "mix_pre_g": gain(ks[1]),
        "w_in": jax.random.normal(ks[2], (DEPTH, D_MODEL, IN_WIDTH), f32) * D_MODEL ** -0.5,
        "sinks": 0.5 * jax.random.normal(ks[3], (DEPTH, N_Q_HEADS), f32),
        "conv_w": jax.random.normal(ks[4], (DEPTH, CONV_KERNEL, CONV_WIDTH), f32) * CONV_KERNEL ** -0.5,
        "conv_b": 0.02 * jax.random.normal(ks[5], (DEPTH, CONV_WIDTH), f32),
        "conv_ln_g": 1.0 + 0.02 * jax.random.normal(ks[6], (DEPTH, CONV_WIDTH), f32),
        "conv_ln_b": 0.02 * jax.random.normal(ks[7], (DEPTH, CONV_WIDTH), f32),
        "attn_out_g": 1.0 + 0.02 * jax.random.normal(ks[8], (DEPTH, ATTN_WIDTH), f32),
        "conv_out_g": 1.0 + 0.02 * jax.random.normal(ks[9], (DEPTH, CONV_WIDTH), f32),
        "w_out": jax.random.normal(ks[10], (DEPTH, D_MODEL, D_MODEL), f32) * D_MODEL ** -0.5,
        "mix_post_g": gain(ks[11]),
        "ffn_pre_g": gain(ks[12]),
        "w_gate": jax.random.normal(ks[13], (DEPTH, D_MODEL, D_FF), f32) * D_MODEL ** -0.5,
        "w_up": jax.random.normal(ks[14], (DEPTH, D_MODEL, D_FF), f32) * D_MODEL ** -0.5,
        "w_down": jax.random.normal(ks[15], (DEPTH, D_FF, D_MODEL), f32) * D_FF ** -0.5,
        "ffn_post_g": gain(ks[16]),
    }


def reference(x, positions, mix_pre_g, w_in, sinks, conv_w, conv_b, conv_ln_g, conv_ln_b,
              attn_out_g, conv_out_g, w_out, mix_post_g, ffn_pre_g, w_gate, w_up, w_down,
              ffn_post_g):
    B, S = x.shape[0], x.shape[1]
    cos, sin = rope_tables(positions)
    splits = [ATTN_WIDTH, ATTN_WIDTH + KV_WIDTH, ATTN_WIDTH + 2 * KV_WIDTH,
              ATTN_WIDTH + 2 * KV_WIDTH + CONV_WIDTH]
    for l in range(DEPTH):
        h = rms_norm(x, mix_pre_g[l])
        proj = h @ w_in[l]
        q, k, v, ca, cg = jnp.split(proj, splits, axis=-1)
        q = apply_partial_rope(q.reshape(B, S, N_Q_HEADS, HEAD_DIM), cos, sin)
        k = apply_partial_rope(k.reshape(B, S, N_KV_HEADS, HEAD_DIM), cos, sin)
        v = v.reshape(B, S, N_KV_HEADS, HEAD_DIM)
        attn = windowed_gqa_with_sink(q, k, v, sinks[l])
        conv = conformer_conv(ca, cg, conv_w[l], conv_b[l], conv_ln_g[l], conv_ln_b[l])
        merged = jnp.concatenate([rms_norm(attn, attn_out_g[l]),
                                  rms_norm(conv, conv_out_g[l])], axis=-1)
        x = x + rms_norm(merged @ w_out[l], mix_post_g[l])
        h = rms_norm(x, ffn_pre_g[l])
        f = (jax.nn.silu(h @ w_gate[l]) * (h @ w_up[l])) @ w_down[l]
        x = x + rms_norm(f, ffn_post_g[l])
    return x
```

```python
import math
import numpy as np
import ml_dtypes
import concourse.bass as bass
import concourse.mybir as mybir
from concourse.bass_utils import run_bass_kernel_spmd

F32 = mybir.dt.float32
BF16 = mybir.dt.bfloat16
I32 = mybir.dt.int32
ALU = mybir.AluOpType
AF = mybir.ActivationFunctionType
AX = mybir.AxisListType

D = 4096
NCH = 32
TT = 512
TE = 640
NOWN = 1024
NLOC = 1152
DFF = 11008
NFC = 86
INW = 7168
RMS_EPS = 1e-6
LN_EPS = 1e-5
SB_BASE = 16640
SB_END = 229376
CELL = 256
NEG = -30000.0
ROPE_ENG = "dve"
N_POOL_CONV = 0
NFLUSH = 20


def _esize(dt):
    return 2 if dt == BF16 else 4


class View:
    __slots__ = ("ap", "rg")

    def __init__(self, ap, rg):
        self.ap = ap
        self.rg = rg


class SB:
    def __init__(self, nc, name, shape, dtype, off):
        assert off % 32 == 0 and off >= SB_BASE, (name, off)
        self.shape = list(shape)
        self.es = _esize(dtype)
        self.off = off
        fs = self.shape[1:]
        st = [1] * len(fs)
        for i in range(len(fs) - 2, -1, -1):
            st[i] = st[i + 1] * fs[i + 1]
        self.st = st
        self.nbytes = self.es * int(np.prod(fs))
        assert off + self.nbytes <= SB_END, (name, off, self.nbytes)
        self.h = nc.alloc_sbuf_tensor_at(name, self.shape, dtype, offset=off)
        self.space = "sb"

    def __call__(self, p=slice(None), *f):
        f = list(f) + [slice(None)] * (len(self.shape) - 1 - len(f))
        lo = 0
        hi = 0
        for i, s in enumerate(f):
            n = self.shape[1 + i]
            if isinstance(s, int):
                a, b = s, s + 1
            else:
                a = 0 if s.start is None else s.start
                b = n if s.stop is None else s.stop
            assert 0 <= a < b <= n, (s, n)
            lo += a * self.st[i]
            hi += (b - 1) * self.st[i]
        hi += 1
        ap = self.h[(p,) + tuple(f)]
        return View(ap, (self.space, self.off + lo * self.es, self.off + hi * self.es))


class PS:
    def __init__(self, nc):
        self.h = nc.alloc_psum_tensor("ps", [128, 8, 512], F32)
        self.hb = self.h.bitcast(BF16)

    def f(self, p, b, c0, c1):
        if isinstance(b, int):
            b0, b1 = b, b + 1
        else:
            b0, b1 = b.start, b.stop
        return View(self.h[p, b, c0:c1], ("ps", b0 * 2048 + c0 * 4, (b1 - 1) * 2048 + c1 * 4))

    def b(self, p, b, c0, c1):
        if isinstance(b, int):
            b0, b1 = b, b + 1
        else:
            b0, b1 = b.start, b.stop
        return View(self.hb[p, b, c0:c1], ("ps", b0 * 2048 + c0 * 2, (b1 - 1) * 2048 + c1 * 2))


class Prog:
    ENG = ("pe", "act", "dve", "pool", "sp")

    def __init__(self, nc):
        self.nc = nc
        self.q = {e: [] for e in self.ENG}
        self.esem = {e: nc.alloc_semaphore("s_" + e) for e in ("pe", "act", "dve", "pool")}
        self.ecnt = {e: 0 for e in self.esem}
        self.waited = {e: {} for e in self.ENG}
        ncell = (SB_END + CELL - 1) // CELL
        self.cw = {"sb": [None] * ncell, "ps": [None] * 64, "dr": {}}
        self.cr = {"sb": [None] * ncell, "ps": [None] * 64, "dr": {}}
        self.chan = {}
        self.nops = 0

    def _cells(self, rg):
        sp, lo, hi = rg
        if sp == "dr":
            return sp, range(lo, hi)
        return sp, range(lo // CELL, (hi - 1) // CELL + 1)

    def _need(self, reads, writes):
        need = {}

        def add(d):
            if d:
                for k, hv in d.items():
                    o = need.get(k)
                    if o is None or o[1] < hv[1]:
                        need[k] = hv
        for rg in reads:
            sp, cells = self._cells(rg)
            cw = self.cw[sp]
            for c in cells:
                add(cw[c] if sp != "dr" else cw.get(c))
        for rg in writes:
            sp, cells = self._cells(rg)
            cw = self.cw[sp]
            cr = self.cr[sp]
            for c in cells:
                if sp != "dr":
                    add(cw[c])
                    add(cr[c])
                else:
                    add(cw.get(c))
                    add(cr.get(c))
        return need

    def _commit(self, key, hv, reads, writes):
        for rg in writes:
            sp, cells = self._cells(rg)
            cw = self.cw[sp]
            for c in cells:
                d = cw[c] if sp != "dr" else cw.get(c)
                if d is None:
                    d = {}
                    cw[c] = d
                d[key] = hv
        for rg in reads:
            sp, cells = self._cells(rg)
            cr = self.cr[sp]
            for c in cells:
                d = cr[c] if sp != "dr" else cr.get(c)
                if d is None:
                    d = {}
                    cr[c] = d
                d[key] = hv

    def _waits(self, eng, need):
        waits = []
        wd = self.waited[eng]
        for k, (h, v) in need.items():
            if eng == "pe" and k == "pe":
                continue
            if wd.get(k, 0) >= v:
                continue
            wd[k] = v
            waits.append((h, v))
        return waits

    @staticmethod
    def _rgs(vs):
        return [v.rg if isinstance(v, View) else v for v in vs]

    def op(self, eng, fn, reads=(), writes=()):
        reads = self._rgs(reads)
        writes = self._rgs(writes)
        waits = self._waits(eng, self._need(reads, writes))
        self.ecnt[eng] += 1
        hv = (self.esem[eng], self.ecnt[eng])
        self.q[eng].append((waits, fn, self.esem[eng], 1))
        self._commit(eng, hv, reads, writes)
        self.nops += 1

    def dma(self, queue, chan, fn, reads=(), writes=()):
        reads = self._rgs(reads)
        writes = self._rgs(writes)
        if chan not in self.chan:
            self.chan[chan] = [self.nc.alloc_semaphore("c_" + chan), 0]
        c = self.chan[chan]
        waits = self._waits(queue, self._need(reads, writes))
        c[1] += 16
        hv = (c[0], c[1])
        self.q[queue].append((waits, fn, c[0], 16))
        self._commit("c_" + chan, hv, reads, writes)
        self.nops += 1

    def emit(self, final_chans):
        nc = self.nc
        q = self.q
        chan = self.chan

        def run(e, name):
            for waits, fn, sem, inc in q[name]:
                for h, v in waits:
                    e.wait_ge(h, v)
                ins = fn(e)
                ins.then_inc(sem, inc)
            if name == "sp":
                for cn in final_chans:
                    if cn in chan:
                        e.wait_ge(chan[cn][0], chan[cn][1])

        with nc.Block() as block:
            @block.tensor
            def _(e):
                run(e, "pe")

            @block.scalar
            def _(e):
                run(e, "act")

            @block.vector
            def _(e):
                run(e, "dve")

            @block.gpsimd
            def _(e):
                run(e, "pool")

            @block.sync
            def _(e):
                run(e, "sp")


def DR(key):
    return ("dr", key, key + 1)


def build_program(stop_after=None, ntiles=2):
    nc = bass.Bass("TRN2", target_bir_lowering=False)
    P = Prog(nc)
    dbg = {}

    def din(name, shape, dt=F32):
        return nc.dram_tensor(name, list(shape), dt, kind="ExternalInput")

    x_d = din("x", [NLOC, D])
    pos_d = din("pos", [1, NLOC], I32)
    w_in_d = din("w_in", [D, INW])
    w_out_d = din("w_out", [D, D])
    w_gate_d = din("w_gate", [D, DFF])
    w_up_d = din("w_up", [D, DFF])
    w_down_d = din("w_down", [DFF, D])
    g1T_d = din("g1T", [128, NCH])
    g2T_d = din("g2T", [128, NCH])
    cvec_d = din("cvec", [128, 64])
    convw_d = din("convw", [128, 16 * 31])
    sinks_d = din("sinks", [1, 16])
    attn_g_d = din("attn_g", [1, 2048])
    post1_d = din("post1", [1, D])
    post2_d = din("post2", [1, D])
    ident_d = din("ident", [128, 128], BF16)
    maskb_d = din("maskb", [128, 384])
    ropec_d = din("ropec", [128, 4])
    y_d = nc.dram_tensor("y", [NOWN, D], F32, kind="ExternalOutput")

    o = SB_BASE
    def alloc(name, shape, dt, off=None):
        nonlocal o
        if off is None:
            off = o
            t = SB(nc, name, shape, dt, off)
            o = (off + t.nbytes + 255) // 256 * 256
            return t
        return SB(nc, name, shape, dt, off)

    ident = alloc("ident", [128, 128], BF16)
    ones = alloc("ones", [128, 128], F32)
    maskb = alloc("maskb", [128, 384], F32)
    convw = alloc("convw", [128, 16, 31], F32)
    g1T = alloc("g1T", [128, NCH], F32)
    g2T = alloc("g2T", [128, NCH], F32)
    cvec = alloc("cvec", [128, 4, 16], F32)
    sinkb = alloc("sinkb", [128, 16], F32)
    ropec = alloc("ropec", [128, 4], F32)
    KT = alloc("KT", [128, 4, 768], BF16)
    VA = alloc("VA", [128, 6, 512], BF16)
    uprev = alloc("uprev", [128, 16, 32], F32)
    st = alloc("st", [128, 64], F32)
    st2 = alloc("st2", [128, 2, 32], F32)
    W0 = o
    NS = 4
    wslot = [alloc("w%d" % i, [128, 8, 512], BF16) for i in range(NS)]
    R2 = o
    o = R2 + 65536
    R1 = o
    R1SZ = SB_END - R1
    assert R1SZ >= 88064 + 3072, R1SZ

    hT = alloc("hT", [128, NCH, TE], BF16, R2)
    mergedT = alloc("mergedT", [128, NCH, TT], BF16, R2)
    h2T = alloc("h2T", [128, NCH, TT], BF16, R2)
    f_sb = alloc("f_sb", [128, 4, D], F32, R2)
    QT = alloc("QT", [128, 16, TT], BF16, R2 + 40960)
    attng = alloc("attng", [128, 2048], F32, R2 + 57344)
    gbc1 = alloc("gbc1", [128, D], F32, R2 + 32768)
    sgsb = [alloc("sg%d" % i, [128, 4, 512], F32, R2 + 32768 + 8192 * i) for i in range(2)]
    xt = [alloc("xt%d" % i, [128, D], F32, R1 + 16384 * i) for i in range(2)]
    xs = alloc("xs", [128, D], BF16, R1 + 32768)
    TB = R1 + 40960
    posi = alloc("posi", [32, TE], I32, TB)
    posf = alloc("posf", [32, TE], F32, TB + 2560)
    ang = alloc("ang", [32, TE], F32, TB + 5120)
    Ct = alloc("Ct", [32, TE], F32, TB + 7680)
    St = alloc("St", [32, TE], F32, TB + 10240)
    tki = alloc("tki", [32, TE], I32, R1 + 54272)
    tkf = alloc("tkf", [32, TE], F32, R1 + 54272 + 2560)
    tm = alloc("tm", [32, TE], F32, R1 + 54272 + 5120)
    uT = alloc("uT", [128, 16, 544], F32, R1)
    acc = [alloc("acc%d" % i, [128, 512], F32, R1 + 36864 + 2048 * i) for i in range(2)]
    accp = alloc("accp", [128, 512], F32, R1 + 34816)
    tmpp = alloc("tmpp", [128, 512], F32, R1 + 36864)
    SC = R1 + 54272
    sig = [alloc("sig%d" % i, [128, 4, 528], F32, SC + 8448 * i) for i in range(2)]
    cab = [alloc("cab%d" % i, [128, 4, 528], F32, SC + 16896 + 8448 * i) for i in range(2)]
    rraw = alloc("rraw", [32, 2, TE], F32, R2 + 57344)
    rsw = alloc("rsw", [32, 2, TE], F32, R1 + 88064)
    LS = R1 + 40960
    lnt = [alloc("lnt%d" % i, [128, 512], F32, LS + 2048 * i) for i in range(6)]
    s_sb = [alloc("s_sb%d" % i, [128, 4, 384], F32, SC + 6144 * i) for i in range(2)]
    Pb = [alloc("Pb%d" % i, [128, 4, 384], BF16, SC + 12288 + 3072 * i) for i in range(2)]
    PTs = [alloc("PTs%d" % i, [128, 4, 384], BF16, SC + 18432 + 3072 * i) for i in range(2)]
    atok = alloc("atok", [128, 2048], F32, SC + 24576)
    anb = alloc("anb", [128, 2048], BF16, SC + 32768)
    assert SC + 36864 <= SB_END
    r_sb = alloc("r_sb", [128, 4, D], F32, R1)
    xe = alloc("xe", [128, D], F32, R1 + 65536)
    xe2 = alloc("xe2", [128, D], F32, R2 + 49152)
    xsr = [alloc("xsr%d" % i, [128, D], BF16, R1 + 16384 * i) for i in range(4)]
    xse = alloc("xse", [128, D], BF16, R1 + 81920)
    actT = alloc("actT", [128, NFC, TT], BF16, R1)
    gbc2 = alloc("gbc2", [128, D], F32, R1)
    xf = [alloc("xf%d" % i, [128, D], F32, R1 + 16384 * (1 + i)) for i in range(2)]

    ps = PS(nc)
    A = slice(None)

    wq = []

    def wblocks(wd, K, f0, FW):
        nk = K // 128
        r = wd.rearrange("(kc p) f -> p kc f", p=128)
        out = []
        k0 = 0
        while k0 < nk:
            kc = min(8, nk - k0)
            out.append((r[:, k0:k0 + kc, f0:f0 + FW], kc, FW, k0))
            k0 += kc
        return out

    plan = []
    for ti in range(ntiles):
        gl = []
        for j in range(4):
            gl.append(("cg%d" % j, wblocks(w_in_d, D, 5120 + 512 * j, 512)))
            gl.append(("ca%d" % j, wblocks(w_in_d, D, 3072 + 512 * j, 512)))
        gl.append(("k", wblocks(w_in_d, D, 2048, 512)))
        gl.append(("v", wblocks(w_in_d, D, 2560, 512)))
        for j in range(4):
            gl.append(("q%d" % j, wblocks(w_in_d, D, 512 * j, 512)))
        for j in range(8):
            gl.append(("o%d" % j, wblocks(w_out_d, D, 512 * j, 512)))
        for j in range(22):
            fw = 512 if j < 21 else 256
            gl.append(("g%d" % j, wblocks(w_gate_d, D, 512 * j, fw)))
            gl.append(("u%d" % j, wblocks(w_up_d, D, 512 * j, fw)))
        for j in range(8):
            gl.append(("d%d" % j, wblocks(w_down_d, DFF, 512 * j, 512)))
        plan.append(gl)
    allblocks = []
    for gl in plan:
        for name, bl in gl:
            allblocks.extend(bl)
    wstate = {"issued": 0, "used": 0}
    pool_bg = []
    NBG = 6

    def bg_flush(n=None):
        k = len(pool_bg) if n is None else min(n, len(pool_bg))
        for _ in range(k):
            pool_bg.pop(0)()

    def w_issue():
        _w_issue()
        bg_flush(NBG)

    def _w_issue():
        i = wstate["issued"]
        if i >= len(allblocks):
            return
        apd, kc, fw, k0 = allblocks[i]
        sl = wslot[i % NS]
        dst = sl(A, slice(0, kc), slice(0, fw))
        P.dma("pool", "w%d" % (i % NS),
              lambda e, a=apd, d=dst.ap: e.dma_start(out=d, in_=a),
              reads=(), writes=(dst,))
        wstate["issued"] += 1

    def w_next():
        i = wstate["used"]
        apd, kc, fw, k0 = allblocks[i]
        wstate["used"] += 1
        return wslot[i % NS], kc, fw, k0

    def rstd_op(ss_v, n, eps, out_v, tmpcol):
        t1 = st(A, slice(tmpcol, tmpcol + 1))
        t2 = st(A, slice(tmpcol + 1, tmpcol + 2))
        P.op("dve", lambda e: e.tensor_scalar(out=t1.ap, in0=ss_v.ap, scalar1=1.0 / n, scalar2=eps,
                                              op0=ALU.mult, op1=ALU.add), [ss_v], [t1])
        P.op("act", lambda e: e.activation(out=t2.ap, in_=t1.ap, func=AF.Sqrt), [t1], [t2])
        P.op("dve", lambda e: e.reciprocal(out=out_v.ap, in_=t2.ap), [t2], [out_v])

    def to_featmajor(src_bf, dstT, col0, gT, bankset):
        for b4 in range(4):
            bank = bankset * 4 + b4
            pv = ps.b(A, bank, 0, 1024)

            def tr(e, b4=b4, bank=bank):
                ins = None
                for i in range(8):
                    c = b4 * 8 + i
                    ins = e.transpose(ps.hb[:, bank, i * 128:(i + 1) * 128],
                                      src_bf.h[:, c * 128:(c + 1) * 128], ident.h[:, :])
                return ins
            P.op("pe", tr, [src_bf(A, slice(b4 * 1024, b4 * 1024 + 1024)), ident()], [pv])
            dv = dstT(A, slice(b4 * 8, b4 * 8 + 8), slice(col0, col0 + 128))
            gv = gT(A, slice(b4 * 8, b4 * 8 + 8))
            P.op("dve", lambda e, bank=bank, dv=dv, gv=gv: e.tensor_tensor(
                out=dv.ap, in0=ps.hb[:, bank, :].rearrange("p (a b) -> p a b", a=8),
                in1=gv.ap.unsqueeze(2).to_broadcast([128, 8, 128]), op=ALU.mult),
                [pv, gv], [dv])

    def gemm_B(rhsT, pieces, nblk_expected, bankfn):
        nb = nblk_expected
        for bi in range(nb):
            sl, kc, fw, k0 = w_next()
            nj = fw // 128
            first = bi == 0
            last = bi == nb - 1
            reads = [sl(A, slice(0, kc), slice(0, fw)), rhsT(A, slice(k0, k0 + kc))]
            writes = []
            for j in range(nj):
                for (a, n, pi) in pieces:
                    bank, col = bankfn(j, pi)
                    writes.append(ps.f(A, bank, col, col + n))

            def mm(e, sl=sl, kc=kc, nj=nj, k0=k0, first=first, last=last):
                ins = None
                for k in range(kc):
                    for j in range(nj):
                        for (a, n, pi) in pieces:
                            bank, col = bankfn(j, pi)
                            if pi == 0:
                                ins = e.matmul(ps.h[:, bank, col:col + n],
                                               lhsT=sl.h[:, k, j * 128:(j + 1) * 128],
                                               rhs=rhsT.h[:, k0 + k, a:a + n],
                                               start=(first and k == 0), stop=(last and k == kc - 1))
                            else:
                                ins = e.matmul(ps.h[:, bank, col:col + n],
                                               lhsT=sl.h[:, k, j * 128:(j + 1) * 128],
                                               rhs=rhsT.h[:, k0 + k, a:a + n],
                                               start=(first and k == 0 and j == 0),
                                               stop=(last and k == kc - 1 and j == nj - 1),
                                               skip_group_check=True)
                return ins
            P.op("pe", mm, reads, writes)
            w_issue()

    def gemm_A(lhsT_src, tiles, nblk, fw=512):
        for bi in range(nblk):
            sl, kc, fw_, k0 = w_next()
            assert fw_ == fw
            first = bi == 0
            last = bi == nblk - 1
            reads = [sl(A, slice(0, kc), slice(0, fw)), lhsT_src(A, slice(k0, k0 + kc))]
            writes = [ps.f(A, bank, 0, fw) for (tt, bank) in tiles]

            def mm(e, sl=sl, kc=kc, k0=k0, first=first, last=last):
                ins = None
                for k in range(kc):
                    for (tt, bank) in tiles:
                        ins = e.matmul(ps.h[:, bank, 0:fw],
                                       lhsT=lhsT_src.h[:, k0 + k, tt * 128:(tt + 1) * 128],
                                       rhs=sl.h[:, k, 0:fw],
                                       start=(first and k == 0), stop=(last and k == kc - 1))
                return ins
            P.op("pe", mm, reads, writes)
            w_issue()

    def dump(name, sbt, dt=None):
        shape = sbt.shape
        dd = nc.dram_tensor("dbg_" + name, shape, BF16 if sbt.es == 2 else F32, kind="ExternalOutput")
        dbg[name] = dd
        P.dma("sp", "dbg", lambda e: e.dma_start(out=dd[tuple([A] * len(shape))], in_=sbt().ap),
              reads=[sbt()], writes=())

    _ldn = [0]

    def ld(dst, src_ap):
        _ldn[0] += 1
        P.dma("sp", "su%d" % _ldn[0], lambda e: e.dma_start(out=dst.ap, in_=src_ap), (), [dst])

    ld(ident(), ident_d[:, :])
    ld(maskb(), maskb_d[:, :])
    ld(convw(), convw_d.rearrange("p (c k) -> p c k", c=16))
    ld(g1T(), g1T_d[:, :])
    ld(g2T(), g2T_d[:, :])
    ld(cvec(), cvec_d.rearrange("p (a c) -> p a c", a=4))
    ld(sinkb(), sinks_d[0:1, :].partition_broadcast(128))
    ld(ropec(), ropec_d[:, :])
    P.op("dve", lambda e: e.memset(ones().ap, 1.0), (), [ones()])
    P.op("dve", lambda e: e.memset(uprev().ap, 0.0), (), [uprev()])
    P.op("dve", lambda e: e.memset(KT().ap, 0.0), (), [KT()])
    P.op("dve", lambda e: e.memset(VA().ap, 0.0), (), [VA()])
    for _ in range(NS):
        _w_issue()

    inv_sqrt_d = 1.0 / math.sqrt(128.0)
    TWO_PI = 2.0 * math.pi
    final_chans = []
    stopped = False

    for ti in range(ntiles):
        s0 = ti * TT
        gset = [0]

        def next_set():
            p = gset[0] % 2
            gset[0] += 1
            return p

        P.dma("sp", "pos", lambda e, s0=s0: e.dma_start(
            out=posi().ap, in_=pos_d[0:1, s0:s0 + TE].partition_broadcast(32)), (), [posi()])
        P.op("dve", lambda e: e.tensor_copy(out=posf().ap, in_=posi().ap), [posi()], [posf()])
        rc_f = ropec(slice(0, 32), slice(0, 1))
        rc_s = ropec(slice(0, 32), slice(1, 2))
        def sin_table(dst, phase, scale_ap, extra_reads):
            P.op("dve", lambda e: e.tensor_scalar(out=ang().ap, in0=posf().ap, scalar1=rc_f.ap,
                                                  scalar2=phase, op0=ALU.mult, op1=ALU.add),
                 [posf(), rc_f], [ang()])
            P.op("dve", lambda e: e.tensor_scalar(out=tki().ap, in0=ang().ap, scalar1=1.0 / TWO_PI,
                                                  scalar2=None, op0=ALU.mult), [ang()], [tki()])
            P.op("dve", lambda e: e.tensor_copy(out=tkf().ap, in_=tki().ap), [tki()], [tkf()])
            P.op("dve", lambda e: e.scalar_tensor_tensor(out=ang().ap, in0=tkf().ap, scalar=-TWO_PI,
                                                         in1=ang().ap, op0=ALU.mult, op1=ALU.add),
                 [tkf(), ang()], [ang()])
            P.op("dve", lambda e: e.tensor_scalar(out=tm().ap, in0=ang().ap, scalar1=math.pi,
                                                  scalar2=-TWO_PI, op0=ALU.is_gt, op1=ALU.mult),
                 [ang()], [tm()])
            P.op("dve", lambda e: e.tensor_tensor(out=ang().ap, in0=ang().ap, in1=tm().ap, op=ALU.add),
                 [ang(), tm()], [ang()])
            P.op("dve", lambda e: e.tensor_scalar(out=tm().ap, in0=ang().ap, scalar1=-math.pi,
                                                  scalar2=TWO_PI, op0=ALU.is_lt, op1=ALU.mult),
                 [ang()], [tm()])
            P.op("dve", lambda e: e.tensor_tensor(out=ang().ap, in0=ang().ap, in1=tm().ap, op=ALU.add),
                 [ang(), tm()], [ang()])
            P.op("dve", lambda e: e.tensor_scalar(out=ang().ap, in0=ang().ap, scalar1=-math.pi,
                                                  scalar2=math.pi, op0=ALU.max, op1=ALU.min),
                 [ang()], [ang()])
            if scale_ap is None:
                P.op("act", lambda e: e.activation(out=dst().ap, in_=ang().ap, func=AF.Sin), [ang()], [dst()])
            else:
                P.op("act", lambda e: e.activation(out=dst().ap, in_=ang().ap, func=AF.Sin, scale=scale_ap),
                     [ang()] + extra_reads, [dst()])

        sin_table(Ct, 0.5 * math.pi, None, [])
        sin_table(St, 0.0, rc_s.ap, [rc_s])

        def _xtile(m):
            xb = xt[m % 2]
            r0 = s0 + 128 * m
            cb0 = 0 if m % 2 == 0 else 12
            P.dma("sp", "xl%d" % (m % 2), lambda e, xb=xb, r0=r0: e.dma_start(
                out=xb().ap, in_=x_d[r0:r0 + 128, :]), (), [xb()])
            ssv = st(A, slice(cb0, cb0 + 1))
            P.op("act", lambda e: e.activation(out=xse().ap, in_=xb().ap, func=AF.Square,
                                               accum_out=ssv.ap), [xb()], [xse(), ssv])
            rs = st(A, slice(cb0 + 1, cb0 + 2))
            rstd_op(ssv, D, RMS_EPS, rs, cb0 + 2)
            P.op("act", lambda e: e.activation(out=xs().ap, in_=xb().ap, func=AF.Copy, scale=rs.ap),
                 [xb(), rs], [xs()])
            to_featmajor(xs, hT, 128 * m, g1T, m % 2)
        for m in range(5):
            _xtile(m)
        if stop_after == "hT" and ti == ntiles - 1:
            dump("hT", hT)
            stopped = True
            break

        first_tile = (ti == 0)
        P.op("dve", lambda e: e.tensor_copy(out=uT(A, A, slice(0, 32)).ap, in_=uprev().ap),
             [uprev()], [uT(A, A, slice(0, 32))])

        def conv_ops(c):
            a = acc[c % 2]
            wv = convw(A, c)
            bv = cvec(A, 0, slice(c, c + 1))
            ops = []
            ops.append((lambda e: e.tensor_scalar(out=a().ap, in0=uT.h[:, c, 1:513],
                                                  scalar1=convw.h[:, c, 0:1], scalar2=bv.ap,
                                                  op0=ALU.mult, op1=ALU.add),
                        [uT(A, c), wv, bv], [a()]))
            for k in range(1, 31):
                ops.append((lambda e, k=k: e.scalar_tensor_tensor(
                    out=a().ap, in0=uT.h[:, c, k + 1:k + 513], scalar=convw.h[:, c, k:k + 1],
                    in1=a().ap, op0=ALU.mult, op1=ALU.add), [uT(A, c), wv, a()], [a()]))
            ops.append((lambda e: e.tensor_copy(out=uprev(A, c).ap, in_=uT.h[:, c, 512:544]),
                        [uT(A, c)], [uprev(A, c)]))
            ops.append((lambda e: e.tensor_copy(out=uT.h[:, c, 0:512], in_=a().ap),
                        [a()], [uT(A, c)]))
            return ops

        dve_bg = []

        def dve_bg_flush(n=None):
            k = len(dve_bg) if n is None else min(n, len(dve_bg))
            for _ in range(k):
                dve_bg.pop(0)()

        def conv_pair(c0, c1, defer=False):
            o0, o1 = conv_ops(c0), conv_ops(c1)
            for x0, x1 in zip(o0, o1):
                for x in (x0, x1):
                    if defer:
                        dve_bg.append(lambda x=x: P.op("dve", *x))
                    else:
                        P.op("dve", *x)

        def glu_group(j, kind, dstb):
            p = next_set()
            xb_ = 4 * (1 - p)
            pieces = [(16, 512, 0)] + ([(0, 16, 1)] if first_tile else [])
            gemm_B(hT, pieces, 4,
                   lambda jj, pi, p=p, xb_=xb_: (4 * p + jj, 0) if pi == 0 else (xb_, 16 * jj))
            P.op("act", lambda e, p=p: e.activation(
                out=dstb.h[:, :, 16:528], in_=ps.h[:, 4 * p:4 * p + 4, :], func=kind),
                [ps.f(A, slice(4 * p, 4 * p + 4), 0, 512)], [dstb()])
            if first_tile:
                P.op("act", lambda e, xb_=xb_: e.activation(
                    out=dstb.h[:, :, 0:16], in_=ps.h[:, xb_, 0:64].rearrange("p (a b) -> p a b", a=4),
                    func=kind), [ps.f(A, xb_, 0, 64)], [dstb()])

        for j in range(4):
            sgb = sig[j % 2]
            cb = cab[j % 2]
            glu_group(j, AF.Sigmoid, sgb)
            glu_group(j, AF.Copy, cb)
            t0c = 0 if first_tile else 16
            uv = uT(A, slice(4 * j, 4 * j + 4), slice(16 + t0c, 544))
            P.op("dve", lambda e, uv=uv, cb=cb, sgb=sgb, t0c=t0c: e.tensor_tensor(
                out=uv.ap, in0=cb.h[:, :, t0c:528], in1=sgb.h[:, :, t0c:528], op=ALU.mult), [cb(), sgb()], [uv])
            conv_pair(4 * j, 4 * j + 1, defer=(j >= 2))
            conv_pair(4 * j + 2, 4 * j + 3, defer=(j >= 2))

        def rope_batch(src_fn, c0, n, dst_view):
            cs = slice(c0, c0 + n)
            rv = rraw(slice(0, 32), A, cs)
            swv = rsw(slice(0, 32), A, cs)
            src_fn(rv)
            P.dma("sp", "rope", lambda e: e.dma_start(out=rsw.h[0:16, :, cs], in_=rraw.h[16:32, :, cs]),
                  [rv], [swv])
            P.dma("sp", "rope", lambda e: e.dma_start(out=rsw.h[16:32, :, cs], in_=rraw.h[0:16, :, cs]),
                  [rv], [swv])
            cv = Ct(slice(0, 32), cs)
            sv = St(slice(0, 32), cs)
            P.op(ROPE_ENG, lambda e: e.tensor_tensor(out=rv.ap, in0=rv.ap,
                                                     in1=cv.ap.unsqueeze(1).to_broadcast([32, 2, n]),
                                                     op=ALU.mult), [rv, cv], [rv])
            P.op(ROPE_ENG, lambda e: e.tensor_tensor(out=swv.ap, in0=swv.ap,
                                                     in1=sv.ap.unsqueeze(1).to_broadcast([32, 2, n]),
                                                     op=ALU.mult), [swv, sv], [swv])
            P.op(ROPE_ENG, lambda e: e.tensor_tensor(out=dst_view.ap, in0=rv.ap, in1=swv.ap, op=ALU.add),
                 [rv, swv], [dst_view])
            dve_bg_flush(NFLUSH)

        p = next_set()
        xb_ = 4 * (1 - p)
        pieces = [(128, 512, 0)] + ([(0, 128, 1)] if first_tile else [])
        gemm_B(hT, pieces, 4,
               lambda jj, pi, p=p, xb_=xb_: (4 * p + jj, 0) if pi == 0 else (xb_, 128 * jj))
        for (pa, pb) in ((32, 64), (64, 128)):
            P.op("act", lambda e, p=p, pa=pa, pb=pb: e.activation(
                out=KT.h[pa:pb, :, 256:768], in_=ps.h[pa:pb, 4 * p:4 * p + 4, :], func=AF.Copy),
                [ps.f(A, slice(4 * p, 4 * p + 4), 0, 512)], [KT(A, A, slice(256, 768))])
            if first_tile:
                P.op("act", lambda e, xb_=xb_, pa=pa, pb=pb: e.activation(
                    out=KT.h[pa:pb, :, 128:256], in_=ps.h[pa:pb, xb_, :].rearrange("p (a b) -> p a b", a=4),
                    func=AF.Copy), [ps.f(A, xb_, 0, 512)], [KT(A, A, slice(128, 256))])
        for hb in range(2):
            def ksrc(rv, p=p, xb_=xb_, hb=hb):
                P.op("act", lambda e: e.activation(
                    out=rraw.h[0:32, :, 128:640], in_=ps.h[0:32, 4 * p + 2 * hb:4 * p + 2 * hb + 2, :], func=AF.Copy),
                    [ps.f(A, slice(4 * p + 2 * hb, 4 * p + 2 * hb + 2), 0, 512)], [rraw(A, A, slice(128, 640))])
                if first_tile:
                    P.op("act", lambda e: e.activation(
                        out=rraw.h[0:32, :, 0:128],
                        in_=ps.h[0:32, xb_, 256 * hb:256 * hb + 256].rearrange("p (a b) -> p a b", a=2),
                        func=AF.Copy), [ps.f(A, xb_, 256 * hb, 256 * hb + 256)], [rraw(A, A, slice(0, 128))])
            if first_tile:
                rope_batch(ksrc, 0, 640, View(KT.h[0:32, 2 * hb:2 * hb + 2, 128:768],
                                              KT(A, slice(2 * hb, 2 * hb + 2), slice(128, 768)).rg))
            else:
                rope_batch(ksrc, 128, 512, View(KT.h[0:32, 2 * hb:2 * hb + 2, 256:768],
                                                KT(A, slice(2 * hb, 2 * hb + 2), slice(256, 768)).rg))

        p = next_set()
        xb_ = 4 * (1 - p)
        vtiles = [(1, 4 * p), (2, 4 * p + 1), (3, 4 * p + 2), (4, 4 * p + 3)] + ([(0, xb_)] if first_tile else [])
        gemm_A(hT, vtiles, 4)
        P.op("act", lambda e, p=p: e.activation(out=VA.h[:, 2:6, :], in_=ps.h[:, 4 * p:4 * p + 4, :],
                                                func=AF.Copy),
             [ps.f(A, slice(4 * p, 4 * p + 4), 0, 512)], [VA(A, slice(2, 6))])
        if first_tile:
            P.op("act", lambda e, xb_=xb_: e.activation(out=VA.h[:, 1, :], in_=ps.h[:, xb_, :], func=AF.Copy),
                 [ps.f(A, xb_, 0, 512)], [VA(A, 1)])

        dve_bg_flush(NFLUSH)
        for j in range(4):
            p = next_set()
            gemm_B(hT, [(0, 512, 0)], 4, lambda jj, pi, p=p: (4 * p + jj, 0))
            for (pa, pb) in ((32, 64), (64, 128)):
                P.op("act", lambda e, p=p, pa=pa, pb=pb, j=j: e.activation(
                    out=QT.h[pa:pb, 4 * j:4 * j + 4, :], in_=ps.h[pa:pb, 4 * p:4 * p + 4, :], func=AF.Copy),
                    [ps.f(A, slice(4 * p, 4 * p + 4), 0, 512)], [QT(A, slice(4 * j, 4 * j + 4))])
            for hb in range(2):
                def qsrc(rv, p=p, hb=hb):
                    P.op("act", lambda e: e.activation(
                        out=rraw.h[0:32, :, 0:512], in_=ps.h[0:32, 4 * p + 2 * hb:4 * p + 2 * hb + 2, :],
                        func=AF.Copy),
                        [ps.f(A, slice(4 * p + 2 * hb, 4 * p + 2 * hb + 2), 0, 512)], [rraw(A, A, slice(0, 512))])
                h0 = 4 * j + 2 * hb
                rope_batch(qsrc, 0, 512, View(QT.h[0:32, h0:h0 + 2, :], QT(A, slice(h0, h0 + 2)).rg))
        if stop_after == "inproj" and ti == ntiles - 1:
            bg_flush()
            dve_bg_flush()
            dump("QT", QT)
            dump("KT", KT)
            dump("VA", VA)
            dump("uT", uT)
            stopped = True
            break

        bg_flush()
        dve_bg_flush()
        S1 = ps.f(A, 0, 0, 512)
        S2 = ps.f(A, 1, 0, 512)
        sqs = [lnt[0], acc[0]]
        for c in range(16):
            yv = uT(A, c, slice(0, 512))
            sq = sqs[c % 2]
            P.op("pe", lambda e, c=c: e.matmul(ps.h[:, 0, :], lhsT=ones.h[:, :], rhs=uT.h[:, c, 0:512],
                                               start=(c == 0), stop=(c == 15)), [ones(), yv], [S1])
            P.op("act", lambda e, c=c, sq=sq: e.activation(out=sq().ap, in_=uT.h[:, c, 0:512], func=AF.Square),
                 [yv], [sq()])
            P.op("pe", lambda e, c=c, sq=sq: e.matmul(ps.h[:, 1, :], lhsT=ones.h[:, :], rhs=sq.h[:, :],
                                                      start=(c == 0), stop=(c == 15)), [ones(), sq()], [S2])
        mean, msq, var, rstd_ln = lnt[1], lnt[2], lnt[3], lnt[4]
        P.op("act", lambda e: e.mul(out=mean().ap, in_=ps.h[:, 0, :], mul=1.0 / 2048),
             [S1], [mean()])
        P.op("dve", lambda e: e.tensor_tensor(out=msq().ap, in0=mean().ap, in1=mean().ap, op=ALU.mult),
             [mean()], [msq()])
        P.op("dve", lambda e: e.scalar_tensor_tensor(out=var().ap, in0=ps.h[:, 1, :], scalar=1.0 / 2048,
                                                     in1=msq().ap, op0=ALU.mult, op1=ALU.subtract),
             [S2, msq()], [var()])
        P.op("dve", lambda e: e.tensor_scalar(out=var().ap, in0=var().ap, scalar1=LN_EPS, scalar2=None,
                                              op0=ALU.add), [var()], [var()])
        P.op("act", lambda e: e.activation(out=var().ap, in_=var().ap, func=AF.Sqrt), [var()], [var()])
        P.op("dve", lambda e: e.reciprocal(out=rstd_ln().ap, in_=var().ap), [var()], [rstd_ln()])
        S3 = ps.f(A, 2, 0, 512)
        for c in range(16):
            yv = uT(A, c, slice(0, 512))
            P.op("dve", lambda e, c=c: e.tensor_tensor(out=uT.h[:, c, 0:512], in0=uT.h[:, c, 0:512],
                                                       in1=mean().ap, op=ALU.subtract), [yv, mean()], [yv])
            P.op("dve", lambda e, c=c: e.tensor_tensor(out=uT.h[:, c, 0:512], in0=uT.h[:, c, 0:512],
                                                       in1=rstd_ln().ap, op=ALU.mult), [yv, rstd_ln()], [yv])
            gv = cvec(A, 1, slice(c, c + 1))
            bv = cvec(A, 2, slice(c, c + 1))
            P.op("act", lambda e, c=c, gv=gv, bv=bv: e.activation(
                out=uT.h[:, c, 0:512], in_=uT.h[:, c, 0:512], func=AF.Silu, scale=gv.ap, bias=bv.ap),
                [yv, gv, bv], [yv])
            sq = sqs[c % 2]
            P.op("act", lambda e, c=c, sq=sq: e.activation(out=sq().ap, in_=uT.h[:, c, 0:512], func=AF.Square),
                 [yv], [sq()])
            P.op("pe", lambda e, c=c, sq=sq: e.matmul(ps.h[:, 2, :], lhsT=ones.h[:, :], rhs=sq.h[:, :],
                                                      start=(c == 0), stop=(c == 15)), [ones(), sq()], [S3])
        r3 = lnt[5]
        P.op("dve", lambda e: e.tensor_scalar(out=r3().ap, in0=ps.h[:, 2, :], scalar1=1.0 / 2048,
                                              scalar2=RMS_EPS, op0=ALU.mult, op1=ALU.add), [S3], [r3()])
        P.op("act", lambda e: e.activation(out=r3().ap, in_=r3().ap, func=AF.Sqrt), [r3()], [r3()])
        P.op("dve", lambda e: e.reciprocal(out=r3().ap, in_=r3().ap), [r3()], [r3()])
        for c in range(16):
            yv = uT(A, c, slice(0, 512))
            gv = cvec(A, 3, slice(c, c + 1))
            mv = mergedT(A, 16 + c)
            P.op("dve", lambda e, c=c, gv=gv, mv=mv: e.scalar_tensor_tensor(
                out=mv.ap, in0=uT.h[:, c, 0:512], scalar=gv.ap, in1=r3().ap, op0=ALU.mult, op1=ALU.mult),
                [yv, gv, r3()], [mv])

        P.dma("sp", "sa", lambda e: e.dma_start(out=attng().ap, in_=attn_g_d[0:1, :].partition_broadcast(128)),
              (), [attng()])
        groups = [(qb, kv) for qb in range(4) for kv in range(4)]

        def att_scores(gi):
            qb, kv = groups[gi]
            full = not (ti == 0 and qb == 0)
            kc0 = 128 * qb if full else 128
            nk = 384 if full else 256
            pc0 = 0 if full else 128
            for hh in range(4):
                h = 4 * kv + hh
                P.op("pe", lambda e, h=h, hh=hh: e.matmul(
                    ps.h[:, hh, pc0:pc0 + nk], lhsT=QT.h[:, h, 128 * qb:128 * qb + 128],
                    rhs=KT.h[:, kv, kc0:kc0 + nk], start=True, stop=True),
                    [QT(A, h, slice(128 * qb, 128 * qb + 128)), KT(A, kv, slice(kc0, kc0 + nk))],
                    [ps.f(A, hh, pc0, pc0 + nk)])

        def att_sm1(gi):
            qb, kv = groups[gi]
            full = not (ti == 0 and qb == 0)
            pc0 = 0 if full else 128
            nk = 384 - pc0
            b = gi % 2
            sb_ = s_sb[b]
            sv = sb_(A, A, slice(pc0, 384))
            P.op("dve", lambda e: e.scalar_tensor_tensor(
                out=sv.ap, in0=ps.h[:, 0:4, pc0:384], scalar=inv_sqrt_d,
                in1=maskb.h[:, pc0:384].unsqueeze(1).to_broadcast([128, 4, nk]),
                op0=ALU.mult, op1=ALU.add), [ps.f(A, slice(0, 4), pc0, 384), maskb()], [sv])
            mx = st2(A, b, slice(0, 4))
            P.op("dve", lambda e: e.tensor_reduce(out=mx.ap, in_=sv.ap, axis=AX.X, op=ALU.max), [sv], [mx])
            skv = sinkb(A, slice(4 * kv, 4 * kv + 4))
            P.op("dve", lambda e: e.tensor_tensor(out=mx.ap, in0=mx.ap, in1=skv.ap, op=ALU.max), [mx, skv], [mx])
            nmx = st2(A, b, slice(4, 8))
            P.op("dve", lambda e: e.tensor_scalar(out=nmx.ap, in0=mx.ap, scalar1=-1.0, scalar2=None,
                                                  op0=ALU.mult), [mx], [nmx])
            dsk = st2(A, b, slice(8, 12))
            P.op("dve", lambda e: e.tensor_tensor(out=dsk.ap, in0=skv.ap, in1=mx.ap, op=ALU.subtract),
                 [skv, mx], [dsk])

        def att_exps(gi):
            qb, kv = groups[gi]
            full = not (ti == 0 and qb == 0)
            pc0 = 0 if full else 128
            b = gi % 2
            sb_ = s_sb[b]
            pb_ = Pb[b]
            sv = sb_(A, A, slice(pc0, 384))
            nmx = st2(A, b, slice(4, 8))
            dsk = st2(A, b, slice(8, 12))
            rsum = st2(A, b, slice(12, 16))
            for hh in range(4):
                P.op("act", lambda e, hh=hh: e.activation(
                    out=pb_.h[:, hh, pc0:384], in_=sb_.h[:, hh, pc0:384], func=AF.Exp,
                    bias=st2.h[:, b, 4 + hh:5 + hh], accum_out=st2.h[:, b, 12 + hh:13 + hh]),
                    [sv, nmx], [pb_(A, hh), rsum])
            esk = st2(A, b, slice(16, 20))
            P.op("act", lambda e: e.activation(out=esk.ap, in_=dsk.ap, func=AF.Exp), [dsk], [esk])

        def att_sm2(gi):
            b = gi % 2
            rsum = st2(A, b, slice(12, 16))
            esk = st2(A, b, slice(16, 20))
            den = st2(A, b, slice(20, 24))
            P.op("dve", lambda e: e.tensor_tensor(out=den.ap, in0=rsum.ap, in1=esk.ap, op=ALU.add),
                 [rsum, esk], [den])
            rden = st2(A, b, slice(24, 28))
            P.op("dve", lambda e: e.reciprocal(out=rden.ap, in_=den.ap), [den], [rden])

        def att_pv(gi):
            qb, kv = groups[gi]
            full = not (ti == 0 and qb == 0)
            j0 = 0 if full else 1
            pb_ = Pb[gi % 2]
            pt_ = PTs[gi % 2]
            for half in range(2):
                bank = 4 + half
                pv_ = ps.b(A, bank, 0, 768)

                def tr(e, half=half, bank=bank):
                    ins = None
                    for hh2 in range(2):
                        hh = 2 * half + hh2
                        for j in range(j0, 3):
                            ins = e.transpose(ps.hb[:, bank, hh2 * 384 + j * 128: hh2 * 384 + (j + 1) * 128],
                                              pb_.h[:, hh, j * 128:(j + 1) * 128], ident.h[:, :])
                    return ins
                P.op("pe", tr, [pb_(A, slice(2 * half, 2 * half + 2)), ident()], [pv_])
                c_lo = 128 * j0
                P.op("act", lambda e, half=half, bank=bank, c_lo=c_lo: e.activation(
                    out=pt_.h[:, 2 * half:2 * half + 2, c_lo:384],
                    in_=ps.hb[:, bank, 0:768].rearrange("p (a b) -> p a b", a=2)[:, :, c_lo:384], func=AF.Copy),
                    [pv_], [pt_(A, slice(2 * half, 2 * half + 2))])
            ov = ps.f(A, 6, 0, 512)

            def pvmm(e):
                ins = None
                for hh in range(4):
                    for j in range(j0, 3):
                        ins = e.matmul(ps.h[:, 6, hh * 128:(hh + 1) * 128],
                                       lhsT=pt_.h[:, hh, j * 128:(j + 1) * 128],
                                       rhs=VA.h[:, qb + j, kv * 128:(kv + 1) * 128],
                                       start=(j == j0), stop=(j == 2))
                return ins
            P.op("pe", pvmm, [pt_(), VA(A, slice(qb, qb + 3))], [ov])
            rden = st2(A, gi % 2, slice(24, 28))
            av = atok(A, slice(512 * kv, 512 * kv + 512))
            P.op("dve", lambda e: e.tensor_tensor(
                out=av.ap.rearrange("p (a b) -> p a b", a=4),
                in0=ps.h[:, 6, :].rearrange("p (a b) -> p a b", a=4),
                in1=rden.ap.unsqueeze(2).to_broadcast([128, 4, 128]), op=ALU.mult), [ov, rden], [av])

        def att_finish(qb):
            ssv = st(A, slice(40, 41))
            P.op("act", lambda e: e.activation(out=anb().ap, in_=atok().ap, func=AF.Square, accum_out=ssv.ap),
                 [atok()], [anb(), ssv])
            rs = st(A, slice(41, 42))
            rstd_op(ssv, 2048, RMS_EPS, rs, 42)
            P.op("dve", lambda e: e.scalar_tensor_tensor(out=anb().ap, in0=atok().ap, scalar=rs.ap,
                                                         in1=attng().ap, op0=ALU.mult, op1=ALU.mult),
                 [atok(), rs, attng()], [anb()])
            for half in range(2):
                pv_ = ps.b(A, 7, 0, 1024)

                def tr(e, half=half):
                    ins = None
                    for i in range(8):
                        c = half * 8 + i
                        ins = e.transpose(ps.hb[:, 7, i * 128:(i + 1) * 128],
                                          anb.h[:, c * 128:(c + 1) * 128], ident.h[:, :])
                    return ins
                P.op("pe", tr, [anb(), ident()], [pv_])
                mv = mergedT(A, slice(half * 8, half * 8 + 8), slice(128 * qb, 128 * qb + 128))
                P.op("act", lambda e, mv=mv: e.activation(
                    out=mv.ap, in_=ps.hb[:, 7, :].rearrange("p (a b) -> p a b", a=8), func=AF.Copy),
                    [pv_], [mv])

        ng = len(groups)
        att_scores(0)
        att_sm1(0)
        for gi in range(ng):
            if gi + 1 < ng:
                att_scores(gi + 1)
            att_exps(gi)
            if gi + 1 < ng:
                att_sm1(gi + 1)
            att_sm2(gi)
            att_pv(gi)
            if groups[gi][1] == 3:
                att_finish(groups[gi][0])
        if ti + 1 < ntiles:
            P.op("act", lambda e: e.activation(out=KT.h[:, :, 0:256], in_=KT.h[:, :, 512:768], func=AF.Copy),
                 [KT(A, A, slice(512, 768))], [KT(A, A, slice(0, 256))])
            P.op("act", lambda e: e.activation(out=VA.h[:, 0:2, :], in_=VA.h[:, 4:6, :], func=AF.Copy),
                 [VA(A, slice(4, 6))], [VA(A, slice(0, 2))])
        if stop_after == "merged" and ti == ntiles - 1:
            dump("mergedT", mergedT)
            stopped = True
            break

        for j in range(8):
            p = next_set()
            gemm_A(mergedT, [(tt, 4 * p + tt) for tt in range(4)], 4)
            rv = r_sb(A, A, slice(512 * j, 512 * j + 512))
            P.op("act", lambda e, p=p, rv=rv: e.activation(out=rv.ap, in_=ps.h[:, 4 * p:4 * p + 4, :], func=AF.Copy),
                 [ps.f(A, slice(4 * p, 4 * p + 4), 0, 512)], [rv])
        P.dma("sp", "sb1", lambda e: e.dma_start(out=gbc1().ap, in_=post1_d[0:1, :].partition_broadcast(128)),
              (), [gbc1()])
        def _oepi(tt, s0=s0):
            r0 = s0 + 128 * tt
            b = tt % 2
            xeb = xe if b == 0 else xe2
            xsv = xsr[tt]
            c0 = 44 + 8 * b
            P.dma("sp", "xe%d" % b, lambda e: e.dma_start(out=xeb().ap, in_=x_d[r0:r0 + 128, :]), (), [xeb()])
            rv = r_sb(A, tt)
            ssv = st(A, slice(c0, c0 + 1))
            P.op("act", lambda e: e.activation(out=xse().ap, in_=rv.ap, func=AF.Square, accum_out=ssv.ap),
                 [rv], [xse(), ssv])
            rs = st(A, slice(c0 + 1, c0 + 2))
            rstd_op(ssv, D, RMS_EPS, rs, c0 + 2)
            P.op("dve", lambda e: e.scalar_tensor_tensor(out=rv.ap, in0=rv.ap, scalar=rs.ap, in1=gbc1().ap,
                                                         op0=ALU.mult, op1=ALU.mult), [rv, rs, gbc1()], [rv])
            P.op("dve", lambda e: e.tensor_tensor(out=xeb().ap, in0=xeb().ap, in1=rv.ap, op=ALU.add),
                 [xeb(), rv], [xeb()])
            cn = "ys%d" % b
            if cn not in final_chans:
                final_chans.append(cn)
            P.dma("sp", cn, lambda e: e.dma_start(out=y_d[r0:r0 + 128, :], in_=xeb().ap),
                  [xeb()], [DR(r0 // 128)])
            ss2 = st(A, slice(c0 + 4, c0 + 5))
            P.op("act", lambda e: e.activation(out=xse().ap, in_=xeb().ap, func=AF.Square, accum_out=ss2.ap),
                 [xeb()], [xse(), ss2])
            rs2 = st(A, slice(c0 + 5, c0 + 6))
            rstd_op(ss2, D, RMS_EPS, rs2, c0 + 6)
            P.op("act", lambda e: e.activation(out=xsv().ap, in_=xeb().ap, func=AF.Copy, scale=rs2.ap),
                 [xeb(), rs2], [xsv()])
            to_featmajor(xsv, h2T, 128 * tt, g2T, tt % 2)
        for tt in range(4):
            _oepi(tt)
        if stop_after == "h2T" and ti == ntiles - 1:
            dump("h2T", h2T)
            stopped = True
            break

        for j in range(22):
            nj = 4 if j < 21 else 2
            sg = sgsb[j % 2]
            p = next_set()
            gemm_B(h2T, [(0, 512, 0)], 4, lambda jj, pi, p=p: (4 * p + jj, 0))
            P.op("act", lambda e, p=p, sg=sg, nj=nj: e.activation(
                out=sg.h[:, 0:nj, :], in_=ps.h[:, 4 * p:4 * p + nj, :], func=AF.Silu),
                [ps.f(A, slice(4 * p, 4 * p + nj), 0, 512)], [sg(A, slice(0, nj))])
            p = next_set()
            gemm_B(h2T, [(0, 512, 0)], 4, lambda jj, pi, p=p: (4 * p + jj, 0))
            av = actT(A, slice(4 * j, 4 * j + nj))
            P.op("dve", lambda e, p=p, sg=sg, nj=nj, av=av: e.tensor_tensor(
                out=av.ap, in0=ps.h[:, 4 * p:4 * p + nj, :], in1=sg.h[:, 0:nj, :], op=ALU.mult),
                [ps.f(A, slice(4 * p, 4 * p + nj), 0, 512), sg(A, slice(0, nj))], [av])
        if stop_after == "act" and ti == ntiles - 1:
            dump("actT", actT)
            stopped = True
            break

        for j in range(8):
            p = next_set()
            gemm_A(actT, [(tt, 4 * p + tt) for tt in range(4)], 11)
            fv = f_sb(A, A, slice(512 * j, 512 * j + 512))
            P.op("act", lambda e, p=p, fv=fv: e.activation(out=fv.ap, in_=ps.h[:, 4 * p:4 * p + 4, :], func=AF.Copy),
                 [ps.f(A, slice(4 * p, 4 * p + 4), 0, 512)], [fv])
        P.dma("sp", "sb2", lambda e: e.dma_start(out=gbc2().ap, in_=post2_d[0:1, :].partition_broadcast(128)),
              (), [gbc2()])
        def _fepi(tt, s0=s0):
            r0 = s0 + 128 * tt
            xb = xf[tt % 2]
            P.dma("sp", "xf%d" % (tt % 2), lambda e, r0=r0, xb=xb: e.dma_start(out=xb().ap, in_=y_d[r0:r0 + 128, :]),
                  [DR(r0 // 128)], [xb()])
            fv = f_sb(A, tt)
            ssv = st(A, slice(4, 5))
            junk = actT(A, slice(80, 84))
            P.op("act", lambda e, fv=fv: e.activation(out=junk.ap.rearrange("p a b -> p (a b)"), in_=fv.ap[:, 0:2048],
                                                      func=AF.Square, accum_out=ssv.ap), [fv], [junk, ssv])
            ssv2 = st(A, slice(5, 6))
            P.op("act", lambda e, fv=fv: e.activation(out=junk.ap.rearrange("p a b -> p (a b)"), in_=fv.ap[:, 2048:4096],
                                                      func=AF.Square, accum_out=ssv2.ap), [fv], [junk, ssv2])
            sst = st(A, slice(6, 7))
            P.op("dve", lambda e: e.tensor_tensor(out=sst.ap, in0=ssv.ap, in1=ssv2.ap, op=ALU.add), [ssv, ssv2], [sst])
            rs = st(A, slice(7, 8))
            rstd_op(sst, D, RMS_EPS, rs, 8)
            P.op("dve", lambda e, fv=fv: e.scalar_tensor_tensor(out=fv.ap, in0=fv.ap, scalar=rs.ap, in1=gbc2().ap,
                                                                op0=ALU.mult, op1=ALU.mult), [fv, rs, gbc2()], [fv])
            P.op("dve", lambda e, fv=fv, xb=xb: e.tensor_tensor(out=xb().ap, in0=xb().ap, in1=fv.ap, op=ALU.add),
                 [xb(), fv], [xb()])
            cn = "yo%d" % (tt % 2)
            if cn not in final_chans:
                final_chans.append(cn)
            P.dma("sp", cn, lambda e, r0=r0, xb=xb: e.dma_start(out=y_d[r0:r0 + 128, :], in_=xb().ap),
                  [xb()], [DR(r0 // 128)])
        for tt in range(4):
            _fepi(tt)

    if "dbg" in P.chan or stopped:
        final_chans.append("dbg")
    P.emit(final_chans)
    return nc, dbg


def _consts():
    ident = np.eye(128, dtype=np.float32).astype(ml_dtypes.bfloat16)
    q = np.arange(128)[:, None]
    j = np.arange(384)[None, :]
    valid = np.where(j < 128, j >= q, np.where(j < 256, True, (j - 256) <= q))
    maskb = np.where(valid, 0.0, NEG).astype(np.float32)
    invf = (500000.0 ** (-np.arange(0, 32, 2, dtype=np.float32) / np.float32(32))).astype(np.float32)
    ropec = np.zeros((128, 4), np.float32)
    ropec[0:16, 0] = invf
    ropec[16:32, 0] = invf
    ropec[0:16, 1] = -1.0
    ropec[16:32, 1] = 1.0
    return ident, maskb, ropec


def _core_maps(inp):
    x = np.asarray(inp["x"], dtype=np.float32)
    pos = np.asarray(inp["positions"]).astype(np.int32)
    ident, maskb, ropec = _consts()

    def gT(v):
        return np.ascontiguousarray(np.asarray(v, np.float32).reshape(-1, 128).T)

    common = {
        "w_in": np.asarray(inp["w_in"], np.float32)[0],
        "w_out": np.asarray(inp["w_out"], np.float32)[0],
        "w_gate": np.asarray(inp["w_gate"], np.float32)[0],
        "w_up": np.asarray(inp["w_up"], np.float32)[0],
        "w_down": np.asarray(inp["w_down"], np.float32)[0],
        "g1T": gT(inp["mix_pre_g"][0]),
        "g2T": gT(inp["ffn_pre_g"][0]),
        "cvec": np.ascontiguousarray(np.concatenate(
            [gT(inp["conv_b"][0]), gT(inp["conv_ln_g"][0]), gT(inp["conv_ln_b"][0]), gT(inp["conv_out_g"][0])],
            axis=1)),
        "sinks": np.asarray(inp["sinks"], np.float32).reshape(1, 16),
        "attn_g": np.asarray(inp["attn_out_g"], np.float32).reshape(1, 2048),
        "post1": np.asarray(inp["mix_post_g"], np.float32).reshape(1, D),
        "post2": np.asarray(inp["ffn_post_g"], np.float32).reshape(1, D),
        "ident": ident, "maskb": maskb, "ropec": ropec,
    }
    cw = np.asarray(inp["conv_w"], np.float32)[0]
    def cwdev(c):
        return np.ascontiguousarray(c.T.reshape(16, 128, 31).transpose(1, 0, 2).reshape(128, 16 * 31))
    cw_f = cwdev(cw)
    cw_r = cwdev(cw[::-1])
    maps = []
    idxs = []
    for c in range(8):
        b, half = c // 2, c % 2
        if half == 0:
            idx = np.arange(0, NLOC)
        else:
            idx = 2047 - np.arange(0, NLOC)
        idxs.append((b, idx))
        m = dict(common)
        m["x"] = np.ascontiguousarray(x[b, idx])
        m["pos"] = np.ascontiguousarray(pos[b, idx]).reshape(1, NLOC)
        m["convw"] = cw_f if half == 0 else cw_r
        maps.append(m)
    return maps, idxs


_NC_CACHE = {}


def kernel(**inputs):
    maps, idxs = _core_maps(inputs)
    if "nc" not in _NC_CACHE:
        _NC_CACHE["nc"] = build_program()[0]
    nc = _NC_CACHE["nc"]
    res = run_bass_kernel_spmd(nc, maps, core_ids=list(range(8)))
    out = np.empty((4, 2048, D), np.float32)
    for c in range(8):
        b, idx = idxs[c]
        out[b, idx[:NOWN]] = np.asarray(res.results[c]["y"], np.float32)
    return out
```

```python
import math
import numpy as np
import ml_dtypes
import concourse.bass as bass
import concourse.mybir as mybir
from concourse.bass_utils import run_bass_kernel_spmd

F32 = mybir.dt.float32
BF16 = mybir.dt.bfloat16
I32 = mybir.dt.int32
ALU = mybir.AluOpType
AF = mybir.ActivationFunctionType
AX = mybir.AxisListType

D = 4096
NCH = 32
TT = 512
TE = 640
NOWN = 1024
NLOC = 1152
DFF = 11008
NFC = 86
INW = 7168
RMS_EPS = 1e-6
LN_EPS = 1e-5
SB_BASE = 16640
SB_END = 229376
CELL = 256
NEG = -30000.0
ROPE_ENG = "dve"
N_POOL_CONV = 0
NFLUSH = 21


def _esize(dt):
    return 2 if dt == BF16 else 4


class View:
    __slots__ = ("ap", "rg")

    def __init__(self, ap, rg):
        self.ap = ap
        self.rg = rg


class SB:
    def __init__(self, nc, name, shape, dtype, off):
        assert off % 32 == 0 and off >= SB_BASE, (name, off)
        self.shape = list(shape)
        self.es = _esize(dtype)
        self.off = off
        fs = self.shape[1:]
        st = [1] * len(fs)
        for i in range(len(fs) - 2, -1, -1):
            st[i] = st[i + 1] * fs[i + 1]
        self.st = st
        self.nbytes = self.es * int(np.prod(fs))
        assert off + self.nbytes <= SB_END, (name, off, self.nbytes)
        self.h = nc.alloc_sbuf_tensor_at(name, self.shape, dtype, offset=off)
        self.space = "sb"

    def __call__(self, p=slice(None), *f):
        f = list(f) + [slice(None)] * (len(self.shape) - 1 - len(f))
        lo = 0
        hi = 0
        for i, s in enumerate(f):
            n = self.shape[1 + i]
            if isinstance(s, int):
                a, b = s, s + 1
            else:
                a = 0 if s.start is None else s.start
                b = n if s.stop is None else s.stop
            assert 0 <= a < b <= n, (s, n)
            lo += a * self.st[i]
            hi += (b - 1) * self.st[i]
        hi += 1
        ap = self.h[(p,) + tuple(f)]
        return View(ap, (self.space, self.off + lo * self.es, self.off + hi * self.es))


class PS:
    def __init__(self, nc):
        self.h = nc.alloc_psum_tensor("ps", [128, 8, 512], F32)
        self.hb = self.h.bitcast(BF16)

    def f(self, p, b, c0, c1):
        if isinstance(b, int):
            b0, b1 = b, b + 1
        else:
            b0, b1 = b.start, b.stop
        return View(self.h[p, b, c0:c1], ("ps", b0 * 2048 + c0 * 4, (b1 - 1) * 2048 + c1 * 4))

    def b(self, p, b, c0, c1):
        if isinstance(b, int):
            b0, b1 = b, b + 1
        else:
            b0, b1 = b.start, b.stop
        return View(self.hb[p, b, c0:c1], ("ps", b0 * 2048 + c0 * 2, (b1 - 1) * 2048 + c1 * 2))


class Prog:
    ENG = ("pe", "act", "dve", "pool", "sp")

    def __init__(self, nc):
        self.nc = nc
        self.q = {e: [] for e in self.ENG}
        self.esem = {e: nc.alloc_semaphore("s_" + e) for e in ("pe", "act", "dve", "pool")}
        self.ecnt = {e: 0 for e in self.esem}
        self.waited = {e: {} for e in self.ENG}
        ncell = (SB_END + CELL - 1) // CELL
        self.cw = {"sb": [None] * ncell, "ps": [None] * 64, "dr": {}}
        self.cr = {"sb": [None] * ncell, "ps": [None] * 64, "dr": {}}
        self.chan = {}
        self.nops = 0

    def _cells(self, rg):
        sp, lo, hi = rg
        if sp == "dr":
            return sp, range(lo, hi)
        return sp, range(lo // CELL, (hi - 1) // CELL + 1)

    def _need(self, reads, writes):
        need = {}

        def add(d):
            if d:
                for k, hv in d.items():
                    o = need.get(k)
                    if o is None or o[1] < hv[1]:
                        need[k] = hv
        for rg in reads:
            sp, cells = self._cells(rg)
            cw = self.cw[sp]
            for c in cells:
                add(cw[c] if sp != "dr" else cw.get(c))
        for rg in writes:
            sp, cells = self._cells(rg)
            cw = self.cw[sp]
            cr = self.cr[sp]
            for c in cells:
                if sp != "dr":
                    add(cw[c])
                    add(cr[c])
                else:
                    add(cw.get(c))
                    add(cr.get(c))
        return need

    def _commit(self, key, hv, reads, writes):
        for rg in writes:
            sp, cells = self._cells(rg)
            cw = self.cw[sp]
            for c in cells:
                d = cw[c] if sp != "dr" else cw.get(c)
                if d is None:
                    d = {}
                    cw[c] = d
                d[key] = hv
        for rg in reads:
            sp, cells = self._cells(rg)
            cr = self.cr[sp]
            for c in cells:
                d = cr[c] if sp != "dr" else cr.get(c)
                if d is None:
                    d = {}
                    cr[c] = d
                d[key] = hv

    def _waits(self, eng, need):
        waits = []
        wd = self.waited[eng]
        for k, (h, v) in need.items():
            if eng == "pe" and k == "pe":
                continue
            if wd.get(k, 0) >= v:
                continue
            wd[k] = v
            waits.append((h, v))
        return waits

    @staticmethod
    def _rgs(vs):
        return [v.rg if isinstance(v, View) else v for v in vs]

    def op(self, eng, fn, reads=(), writes=()):
        reads = self._rgs(reads)
        writes = self._rgs(writes)
        waits = self._waits(eng, self._need(reads, writes))
        self.ecnt[eng] += 1
        hv = (self.esem[eng], self.ecnt[eng])
        self.q[eng].append((waits, fn, self.esem[eng], 1))
        self._commit(eng, hv, reads, writes)
        self.nops += 1

    def dma(self, queue, chan, fn, reads=(), writes=()):
        reads = self._rgs(reads)
        writes = self._rgs(writes)
        if chan not in self.chan:
            self.chan[chan] = [self.nc.alloc_semaphore("c_" + chan), 0]
        c = self.chan[chan]
        waits = self._waits(queue, self._need(reads, writes))
        c[1] += 16
        hv = (c[0], c[1])
        self.q[queue].append((waits, fn, c[0], 16))
        self._commit("c_" + chan, hv, reads, writes)
        self.nops += 1

    def emit(self, final_chans):
        nc = self.nc
        q = self.q
        chan = self.chan

        def run(e, name):
            for waits, fn, sem, inc in q[name]:
                for h, v in waits:
                    e.wait_ge(h, v)
                ins = fn(e)
                ins.then_inc(sem, inc)
            if name == "sp":
                for cn in final_chans:
                    if cn in chan:
                        e.wait_ge(chan[cn][0], chan[cn][1])

        with nc.Block() as block:
            @block.tensor
            def _(e):
                run(e, "pe")

            @block.scalar
            def _(e):
                run(e, "act")

            @block.vector
            def _(e):
                run(e, "dve")

            @block.gpsimd
            def _(e):
                run(e, "pool")

            @block.sync
            def _(e):
                run(e, "sp")


def DR(key):
    return ("dr", key, key + 1)


def build_program(stop_after=None, ntiles=2):
    nc = bass.Bass("TRN2", target_bir_lowering=False)
    P = Prog(nc)
    dbg = {}

    def din(name, shape, dt=F32):
        return nc.dram_tensor(name, list(shape), dt, kind="ExternalInput")

    x_d = din("x", [NLOC, D])
    pos_d = din("pos", [1, NLOC], I32)
    w_in_d = din("w_in", [D, INW])
    w_out_d = din("w_out", [D, D])
    w_gate_d = din("w_gate", [D, DFF])
    w_up_d = din("w_up", [D, DFF])
    w_down_d = din("w_down", [DFF, D])
    g1T_d = din("g1T", [128, NCH])
    g2T_d = din("g2T", [128, NCH])
    cvec_d = din("cvec", [128, 64])
    convw_d = din("convw", [128, 16 * 31])
    sinks_d = din("sinks", [1, 16])
    attn_g_d = din("attn_g", [1, 2048])
    post1_d = din("post1", [1, D])
    post2_d = din("post2", [1, D])
    ident_d = din("ident", [128, 128], BF16)
    maskb_d = din("maskb", [128, 384])
    ropec_d = din("ropec", [128, 4])
    y_d = nc.dram_tensor("y", [NOWN, D], F32, kind="ExternalOutput")

    o = SB_BASE
    def alloc(name, shape, dt, off=None):
        nonlocal o
        if off is None:
            off = o
            t = SB(nc, name, shape, dt, off)
            o = (off + t.nbytes + 255) // 256 * 256
            return t
        return SB(nc, name, shape, dt, off)

    ident = alloc("ident", [128, 128], BF16)
    ones = alloc("ones", [128, 128], F32)
    maskb = alloc("maskb", [128, 384], F32)
    convw = alloc("convw", [128, 16, 31], F32)
    g1T = alloc("g1T", [128, NCH], F32)
    g2T = alloc("g2T", [128, NCH], F32)
    cvec = alloc("cvec", [128, 4, 16], F32)
    sinkb = alloc("sinkb", [128, 16], F32)
    ropec = alloc("ropec", [128, 4], F32)
    KT = alloc("KT", [128, 4, 768], BF16)
    VA = alloc("VA", [128, 6, 512], BF16)
    uprev = alloc("uprev", [128, 16, 32], F32)
    st = alloc("st", [128, 64], F32)
    st2 = alloc("st2", [128, 2, 64], F32)
    W0 = o
    NS = 4
    wslot = [alloc("w%d" % i, [128, 8, 512], BF16) for i in range(NS)]
    R2 = o
    o = R2 + 65536
    R1 = o
    R1SZ = SB_END - R1
    assert R1SZ >= 88064 + 3072, R1SZ

    hT = alloc("hT", [128, NCH, TE], BF16, R2)
    mergedT = alloc("mergedT", [128, NCH, TT], BF16, R2)
    h2T = alloc("h2T", [128, NCH, TT], BF16, R2)
    f_sb = alloc("f_sb", [128, 4, D], F32, R2)
    QT = alloc("QT", [128, 16, TT], BF16, R2 + 40960)
    attng = alloc("attng", [128, 2048], F32, R2 + 57344)
    gbc1 = alloc("gbc1", [128, D], F32, R2 + 32768)
    sgsb = [alloc("sg%d" % i, [128, 4, 512], F32, R2 + 32768 + 8192 * i) for i in range(2)]
    xt = [alloc("xt%d" % i, [128, D], F32, R1 + 16384 * i) for i in range(2)]
    xs = alloc("xs", [128, D], BF16, R1 + 32768)
    TB = R1 + 40960
    posi = alloc("posi", [32, TE], I32, TB)
    posf = alloc("posf", [32, TE], F32, TB + 2560)
    ang = alloc("ang", [32, TE], F32, TB + 5120)
    Ct = alloc("Ct", [32, TE], F32, TB + 7680)
    St = alloc("St", [32, TE], F32, TB + 10240)
    tki = alloc("tki", [32, TE], I32, R1 + 54272)
    tkf = alloc("tkf", [32, TE], F32, R1 + 54272 + 2560)
    tm = alloc("tm", [32, TE], F32, R1 + 54272 + 5120)
    uT = alloc("uT", [128, 16, 544], F32, R1)
    acc = [alloc("acc%d" % i, [128, 512], F32, R1 + 36864 + 2048 * i) for i in range(2)]
    accp = alloc("accp", [128, 512], F32, R1 + 34816)
    tmpp = alloc("tmpp", [128, 512], F32, R1 + 36864)
    SC = R1 + 54272
    sig = [alloc("sig%d" % i, [128, 4, 528], F32, SC + 8448 * i) for i in range(2)]
    cab = [alloc("cab%d" % i, [128, 4, 528], F32, SC + 16896 + 8448 * i) for i in range(2)]
    rraw = alloc("rraw", [32, 2, TE], F32, R2 + 57344)
    rsw = alloc("rsw", [32, 2, TE], F32, R1 + 88064)
    LS = R1 + 40960
    lnt = [alloc("lnt%d" % i, [128, 512], F32, LS + 2048 * i) for i in range(6)]
    s_sb = [alloc("s_sb%d" % i, [128, 4, 384], F32, SC + 6144 * i) for i in range(2)]
    Pb = [alloc("Pb%d" % i, [128, 4, 384], BF16, SC + 12288 + 3072 * i) for i in range(2)]
    PTs = [alloc("PTs%d" % i, [128, 4, 384], BF16, SC + 18432 + 3072 * i) for i in range(2)]
    atok = alloc("atok", [128, 2048], F32, SC + 24576)
    anb = alloc("anb", [128, 2048], BF16, SC + 32768)
    assert SC + 36864 <= SB_END
    r_sb = alloc("r_sb", [128, 4, D], F32, R1)
    xe = alloc("xe", [128, D], F32, R1 + 65536)
    xe2 = alloc("xe2", [128, D], F32, R2 + 49152)
    xsr = [alloc("xsr%d" % i, [128, D], BF16, R1 + 16384 * i) for i in range(4)]
    xse = alloc("xse", [128, D], BF16, R1 + 81920)
    actT = alloc("actT", [128, NFC, TT], BF16, R1)
    gbc2 = alloc("gbc2", [128, D], F32, R1)
    xf = [alloc("xf%d" % i, [128, D], F32, R1 + 16384 * (1 + i)) for i in range(2)]

    ps = PS(nc)
    A = slice(None)

    wq = []

    def wblocks(wd, K, f0, FW):
        nk = K // 128
        r = wd.rearrange("(kc p) f -> p kc f", p=128)
        out = []
        k0 = 0
        while k0 < nk:
            kc = min(8, nk - k0)
            out.append((r[:, k0:k0 + kc, f0:f0 + FW], kc, FW, k0))
            k0 += kc
        return out

    plan = []
    for ti in range(ntiles):
        gl = []
        for j in range(4):
            gl.append(("cg%d" % j, wblocks(w_in_d, D, 5120 + 512 * j, 512)))
            gl.append(("ca%d" % j, wblocks(w_in_d, D, 3072 + 512 * j, 512)))
        gl.append(("k", wblocks(w_in_d, D, 2048, 512)))
        gl.append(("v", wblocks(w_in_d, D, 2560, 512)))
        for j in range(4):
            gl.append(("q%d" % j, wblocks(w_in_d, D, 512 * j, 512)))
        for j in range(8):
            gl.append(("o%d" % j, wblocks(w_out_d, D, 512 * j, 512)))
        for j in range(22):
            fw = 512 if j < 21 else 256
            gl.append(("g%d" % j, wblocks(w_gate_d, D, 512 * j, fw)))
            gl.append(("u%d" % j, wblocks(w_up_d, D, 512 * j, fw)))
        for j in range(8):
            gl.append(("d%d" % j, wblocks(w_down_d, DFF, 512 * j, 512)))
        plan.append(gl)
    allblocks = []
    for gl in plan:
        for name, bl in gl:
            allblocks.extend(bl)
    wstate = {"issued": 0, "used": 0}
    pool_bg = []
    NBG = 6

    def bg_flush(n=None):
        k = len(pool_bg) if n is None else min(n, len(pool_bg))
        for _ in range(k):
            pool_bg.pop(0)()

    def w_issue():
        _w_issue()
        bg_flush(NBG)

    def _w_issue():
        i = wstate["issued"]
        if i >= len(allblocks):
            return
        apd, kc, fw, k0 = allblocks[i]
        sl = wslot[i % NS]
        dst = sl(A, slice(0, kc), slice(0, fw))
        P.dma("pool", "w%d" % (i % NS),
              lambda e, a=apd, d=dst.ap: e.dma_start(out=d, in_=a),
              reads=(), writes=(dst,))
        wstate["issued"] += 1

    def w_next():
        i = wstate["used"]
        apd, kc, fw, k0 = allblocks[i]
        wstate["used"] += 1
        return wslot[i % NS], kc, fw, k0

    def rstd_op(ss_v, n, eps, out_v, tmpcol):
        t1 = st(A, slice(tmpcol, tmpcol + 1))
        t2 = st(A, slice(tmpcol + 1, tmpcol + 2))
        P.op("dve", lambda e: e.tensor_scalar(out=t1.ap, in0=ss_v.ap, scalar1=1.0 / n, scalar2=eps,
                                              op0=ALU.mult, op1=ALU.add), [ss_v], [t1])
        P.op("act", lambda e: e.activation(out=t2.ap, in_=t1.ap, func=AF.Sqrt), [t1], [t2])
        P.op("dve", lambda e: e.reciprocal(out=out_v.ap, in_=t2.ap), [t2], [out_v])

    def to_featmajor(src_bf, dstT, col0, gT, bankset):
        for b4 in range(4):
            bank = bankset * 4 + b4
            pv = ps.b(A, bank, 0, 1024)

            def tr(e, b4=b4, bank=bank):
                ins = None
                for i in range(8):
                    c = b4 * 8 + i
                    ins = e.transpose(ps.hb[:, bank, i * 128:(i + 1) * 128],
                                      src_bf.h[:, c * 128:(c + 1) * 128], ident.h[:, :])
                return ins
            P.op("pe", tr, [src_bf(A, slice(b4 * 1024, b4 * 1024 + 1024)), ident()], [pv])
            dv = dstT(A, slice(b4 * 8, b4 * 8 + 8), slice(col0, col0 + 128))
            gv = gT(A, slice(b4 * 8, b4 * 8 + 8))
            P.op("dve", lambda e, bank=bank, dv=dv, gv=gv: e.tensor_tensor(
                out=dv.ap, in0=ps.hb[:, bank, :].rearrange("p (a b) -> p a b", a=8),
                in1=gv.ap.unsqueeze(2).to_broadcast([128, 8, 128]), op=ALU.mult),
                [pv, gv], [dv])

    def gemm_B(rhsT, pieces, nblk_expected, bankfn):
        nb = nblk_expected
        for bi in range(nb):
            sl, kc, fw, k0 = w_next()
            nj = fw // 128
            first = bi == 0
            last = bi == nb - 1
            reads = [sl(A, slice(0, kc), slice(0, fw)), rhsT(A, slice(k0, k0 + kc))]
            writes = []
            for j in range(nj):
                for (a, n, pi) in pieces:
                    bank, col = bankfn(j, pi)
                    writes.append(ps.f(A, bank, col, col + n))

            def mm(e, sl=sl, kc=kc, nj=nj, k0=k0, first=first, last=last):
                ins = None
                for k in range(kc):
                    for j in range(nj):
                        for (a, n, pi) in pieces:
                            bank, col = bankfn(j, pi)
                            if pi == 0:
                                ins = e.matmul(ps.h[:, bank, col:col + n],
                                               lhsT=sl.h[:, k, j * 128:(j + 1) * 128],
                                               rhs=rhsT.h[:, k0 + k, a:a + n],
                                               start=(first and k == 0), stop=(last and k == kc - 1))
                            else:
                                ins = e.matmul(ps.h[:, bank, col:col + n],
                                               lhsT=sl.h[:, k, j * 128:(j + 1) * 128],
                                               rhs=rhsT.h[:, k0 + k, a:a + n],
                                               start=(first and k == 0 and j == 0),
                                               stop=(last and k == kc - 1 and j == nj - 1),
                                               skip_group_check=True)
                return ins
            P.op("pe", mm, reads, writes)
            w_issue()

    def gemm_A(lhsT_src, tiles, nblk, fw=512):
        for bi in range(nblk):
            sl, kc, fw_, k0 = w_next()
            assert fw_ == fw
            first = bi == 0
            last = bi == nblk - 1
            reads = [sl(A, slice(0, kc), slice(0, fw)), lhsT_src(A, slice(k0, k0 + kc))]
            writes = [ps.f(A, bank, 0, fw) for (tt, bank) in tiles]

            def mm(e, sl=sl, kc=kc, k0=k0, first=first, last=last):
                ins = None
                for k in range(kc):
                    for (tt, bank) in tiles:
                        ins = e.matmul(ps.h[:, bank, 0:fw],
                                       lhsT=lhsT_src.h[:, k0 + k, tt * 128:(tt + 1) * 128],
                                       rhs=sl.h[:, k, 0:fw],
                                       start=(first and k == 0), stop=(last and k == kc - 1))
                return ins
            P.op("pe", mm, reads, writes)
            w_issue()

    def dump(name, sbt, dt=None):
        shape = sbt.shape
        dd = nc.dram_tensor("dbg_" + name, shape, BF16 if sbt.es == 2 else F32, kind="ExternalOutput")
        dbg[name] = dd
        P.dma("sp", "dbg", lambda e: e.dma_start(out=dd[tuple([A] * len(shape))], in_=sbt().ap),
              reads=[sbt()], writes=())

    _ldn = [0]

    def ld(dst, src_ap):
        _ldn[0] += 1
        P.dma("sp", "su%d" % _ldn[0], lambda e: e.dma_start(out=dst.ap, in_=src_ap), (), [dst])

    ld(ident(), ident_d[:, :])
    ld(maskb(), maskb_d[:, :])
    ld(convw(), convw_d.rearrange("p (c k) -> p c k", c=16))
    ld(g1T(), g1T_d[:, :])
    ld(g2T(), g2T_d[:, :])
    ld(cvec(), cvec_d.rearrange("p (a c) -> p a c", a=4))
    ld(sinkb(), sinks_d[0:1, :].partition_broadcast(128))
    ld(ropec(), ropec_d[:, :])
    P.op("dve", lambda e: e.memset(ones().ap, 1.0), (), [ones()])
    P.op("dve", lambda e: e.memset(uprev().ap, 0.0), (), [uprev()])
    P.op("dve", lambda e: e.memset(KT().ap, 0.0), (), [KT()])
    P.op("dve", lambda e: e.memset(VA().ap, 0.0), (), [VA()])
    for _ in range(NS):
        _w_issue()

    inv_sqrt_d = 1.0 / math.sqrt(128.0)
    TWO_PI = 2.0 * math.pi
    final_chans = []
    stopped = False

    for ti in range(ntiles):
        s0 = ti * TT
        gset = [0]

        def next_set():
            p = gset[0] % 2
            gset[0] += 1
            return p

        P.dma("sp", "pos", lambda e, s0=s0: e.dma_start(
            out=posi().ap, in_=pos_d[0:1, s0:s0 + TE].partition_broadcast(32)), (), [posi()])
        P.op("dve", lambda e: e.tensor_copy(out=posf().ap, in_=posi().ap), [posi()], [posf()])
        rc_f = ropec(slice(0, 32), slice(0, 1))
        rc_s = ropec(slice(0, 32), slice(1, 2))
        def sin_table(dst, phase, scale_ap, extra_reads):
            P.op("dve", lambda e: e.tensor_scalar(out=ang().ap, in0=posf().ap, scalar1=rc_f.ap,
                                                  scalar2=phase, op0=ALU.mult, op1=ALU.add),
                 [posf(), rc_f], [ang()])
            P.op("dve", lambda e: e.tensor_scalar(out=tki().ap, in0=ang().ap, scalar1=1.0 / TWO_PI,
                                                  scalar2=None, op0=ALU.mult), [ang()], [tki()])
            P.op("dve", lambda e: e.tensor_copy(out=tkf().ap, in_=tki().ap), [tki()], [tkf()])
            P.op("dve", lambda e: e.scalar_tensor_tensor(out=ang().ap, in0=tkf().ap, scalar=-TWO_PI,
                                                         in1=ang().ap, op0=ALU.mult, op1=ALU.add),
                 [tkf(), ang()], [ang()])
            P.op("dve", lambda e: e.tensor_scalar(out=tm().ap, in0=ang().ap, scalar1=math.pi,
                                                  scalar2=-TWO_PI, op0=ALU.is_gt, op1=ALU.mult),
                 [ang()], [tm()])
            P.op("dve", lambda e: e.tensor_tensor(out=ang().ap, in0=ang().ap, in1=tm().ap, op=ALU.add),
                 [ang(), tm()], [ang()])
            P.op("dve", lambda e: e.tensor_scalar(out=tm().ap, in0=ang().ap, scalar1=-math.pi,
                                                  scalar2=TWO_PI, op0=ALU.is_lt, op1=ALU.mult),
                 [ang()], [tm()])
            P.op("dve", lambda e: e.tensor_tensor(out=ang().ap, in0=ang().ap, in1=tm().ap, op=ALU.add),
                 [ang(), tm()], [ang()])
            P.op("dve", lambda e: e.tensor_scalar(out=ang().ap, in0=ang().ap, scalar1=-math.pi,
                                                  scalar2=math.pi, op0=ALU.max, op1=ALU.min),
                 [ang()], [ang()])
            if scale_ap is None:
                P.op("act", lambda e: e.activation(out=dst().ap, in_=ang().ap, func=AF.Sin), [ang()], [dst()])
            else:
                P.op("act", lambda e: e.activation(out=dst().ap, in_=ang().ap, func=AF.Sin, scale=scale_ap),
                     [ang()] + extra_reads, [dst()])

        sin_table(Ct, 0.5 * math.pi, None, [])
        sin_table(St, 0.0, rc_s.ap, [rc_s])

        def _xtile(m):
            xb = xt[m % 2]
            r0 = s0 + 128 * m
            cb0 = 0 if m % 2 == 0 else 12
            P.dma("sp", "xl%d" % (m % 2), lambda e, xb=xb, r0=r0: e.dma_start(
                out=xb().ap, in_=x_d[r0:r0 + 128, :]), (), [xb()])
            ssv = st(A, slice(cb0, cb0 + 1))
            P.op("act", lambda e: e.activation(out=xse().ap, in_=xb().ap, func=AF.Square,
                                               accum_out=ssv.ap), [xb()], [xse(), ssv])
            rs = st(A, slice(cb0 + 1, cb0 + 2))
            rstd_op(ssv, D, RMS_EPS, rs, cb0 + 2)
            P.op("act", lambda e: e.activation(out=xs().ap, in_=xb().ap, func=AF.Copy, scale=rs.ap),
                 [xb(), rs], [xs()])
            to_featmajor(xs, hT, 128 * m, g1T, m % 2)
        for m in range(5):
            _xtile(m)
        if stop_after == "hT" and ti == ntiles - 1:
            dump("hT", hT)
            stopped = True
            break

        first_tile = (ti == 0)
        P.op("dve", lambda e: e.tensor_copy(out=uT(A, A, slice(0, 32)).ap, in_=uprev().ap),
             [uprev()], [uT(A, A, slice(0, 32))])

        def conv_ops(c):
            a = acc[c % 2]
            wv = convw(A, c)
            bv = cvec(A, 0, slice(c, c + 1))
            ops = []
            ops.append((lambda e: e.tensor_scalar(out=a().ap, in0=uT.h[:, c, 1:513],
                                                  scalar1=convw.h[:, c, 0:1], scalar2=bv.ap,
                                                  op0=ALU.mult, op1=ALU.add),
                        [uT(A, c), wv, bv], [a()]))
            for k in range(1, 31):
                ops.append((lambda e, k=k: e.scalar_tensor_tensor(
                    out=a().ap, in0=uT.h[:, c, k + 1:k + 513], scalar=convw.h[:, c, k:k + 1],
                    in1=a().ap, op0=ALU.mult, op1=ALU.add), [uT(A, c), wv, a()], [a()]))
            ops.append((lambda e: e.tensor_copy(out=uprev(A, c).ap, in_=uT.h[:, c, 512:544]),
                        [uT(A, c)], [uprev(A, c)]))
            ops.append((lambda e: e.tensor_copy(out=uT.h[:, c, 0:512], in_=a().ap),
                        [a()], [uT(A, c)]))
            return ops

        dve_bg = []

        def dve_bg_flush(n=None):
            k = len(dve_bg) if n is None else min(n, len(dve_bg))
            for _ in range(k):
                dve_bg.pop(0)()

        def conv_pair(c0, c1, defer=False):
            o0, o1 = conv_ops(c0), conv_ops(c1)
            for x0, x1 in zip(o0, o1):
                for x in (x0, x1):
                    if defer:
                        dve_bg.append(lambda x=x: P.op("dve", *x))
                    else:
                        P.op("dve", *x)

        def glu_group(j, kind, dstb):
            p = next_set()
            xb_ = 4 * (1 - p)
            pieces = [(16, 512, 0)] + ([(0, 16, 1)] if first_tile else [])
            gemm_B(hT, pieces, 4,
                   lambda jj, pi, p=p, xb_=xb_: (4 * p + jj, 0) if pi == 0 else (xb_, 16 * jj))
            P.op("act", lambda e, p=p: e.activation(
                out=dstb.h[:, :, 16:528], in_=ps.h[:, 4 * p:4 * p + 4, :], func=kind),
                [ps.f(A, slice(4 * p, 4 * p + 4), 0, 512)], [dstb()])
            if first_tile:
                P.op("act", lambda e, xb_=xb_: e.activation(
                    out=dstb.h[:, :, 0:16], in_=ps.h[:, xb_, 0:64].rearrange("p (a b) -> p a b", a=4),
                    func=kind), [ps.f(A, xb_, 0, 64)], [dstb()])

        for j in range(4):
            sgb = sig[j % 2]
            cb = cab[j % 2]
            glu_group(j, AF.Sigmoid, sgb)
            glu_group(j, AF.Copy, cb)
            t0c = 0 if first_tile else 16
            uv = uT(A, slice(4 * j, 4 * j + 4), slice(16 + t0c, 544))
            P.op("dve", lambda e, uv=uv, cb=cb, sgb=sgb, t0c=t0c: e.tensor_tensor(
                out=uv.ap, in0=cb.h[:, :, t0c:528], in1=sgb.h[:, :, t0c:528], op=ALU.mult), [cb(), sgb()], [uv])
            conv_pair(4 * j, 4 * j + 1, defer=(j >= 2))
            conv_pair(4 * j + 2, 4 * j + 3, defer=(j >= 2))

        dve_bg_flush(40)
        def rope_batch(src_fn, c0, n, dst_view):
            cs = slice(c0, c0 + n)
            rv = rraw(slice(0, 32), A, cs)
            swv = rsw(slice(0, 32), A, cs)
            src_fn(rv)
            P.dma("sp", "rope", lambda e: e.dma_start(out=rsw.h[0:16, :, cs], in_=rraw.h[16:32, :, cs]),
                  [rv], [swv])
            P.dma("sp", "rope", lambda e: e.dma_start(out=rsw.h[16:32, :, cs], in_=rraw.h[0:16, :, cs]),
                  [rv], [swv])
            cv = Ct(slice(0, 32), cs)
            sv = St(slice(0, 32), cs)
            P.op(ROPE_ENG, lambda e: e.tensor_tensor(out=rv.ap, in0=rv.ap,
                                                     in1=cv.ap.unsqueeze(1).to_broadcast([32, 2, n]),
                                                     op=ALU.mult), [rv, cv], [rv])
            P.op(ROPE_ENG, lambda e: e.tensor_tensor(out=swv.ap, in0=swv.ap,
                                                     in1=sv.ap.unsqueeze(1).to_broadcast([32, 2, n]),
                                                     op=ALU.mult), [swv, sv], [swv])
            P.op(ROPE_ENG, lambda e: e.tensor_tensor(out=dst_view.ap, in0=rv.ap, in1=swv.ap, op=ALU.add),
                 [rv, swv], [dst_view])
            dve_bg_flush(NFLUSH)

        p = next_set()
        xb_ = 4 * (1 - p)
        pieces = [(128, 512, 0)] + ([(0, 128, 1)] if first_tile else [])
        gemm_B(hT, pieces, 4,
               lambda jj, pi, p=p, xb_=xb_: (4 * p + jj, 0) if pi == 0 else (xb_, 128 * jj))
        for (pa, pb) in ((32, 64), (64, 128)):
            P.op("act", lambda e, p=p, pa=pa, pb=pb: e.activation(
                out=KT.h[pa:pb, :, 256:768], in_=ps.h[pa:pb, 4 * p:4 * p + 4, :], func=AF.Copy),
                [ps.f(A, slice(4 * p, 4 * p + 4), 0, 512)], [KT(A, A, slice(256, 768))])
            if first_tile:
                P.op("act", lambda e, xb_=xb_, pa=pa, pb=pb: e.activation(
                    out=KT.h[pa:pb, :, 128:256], in_=ps.h[pa:pb, xb_, :].rearrange("p (a b) -> p a b", a=4),
                    func=AF.Copy), [ps.f(A, xb_, 0, 512)], [KT(A, A, slice(128, 256))])
        for hb in range(2):
            def ksrc(rv, p=p, xb_=xb_, hb=hb):
                P.op("act", lambda e: e.activation(
                    out=rraw.h[0:32, :, 128:640], in_=ps.h[0:32, 4 * p + 2 * hb:4 * p + 2 * hb + 2, :], func=AF.Copy),
                    [ps.f(A, slice(4 * p + 2 * hb, 4 * p + 2 * hb + 2), 0, 512)], [rraw(A, A, slice(128, 640))])
                if first_tile:
                    P.op("act", lambda e: e.activation(
                        out=rraw.h[0:32, :, 0:128],
                        in_=ps.h[0:32, xb_, 256 * hb:256 * hb + 256].rearrange("p (a b) -> p a b", a=2),
                        func=AF.Copy), [ps.f(A, xb_, 256 * hb, 256 * hb + 256)], [rraw(A, A, slice(0, 128))])
            if first_tile:
                rope_batch(ksrc, 0, 640, View(KT.h[0:32, 2 * hb:2 * hb + 2, 128:768],
                                              KT(A, slice(2 * hb, 2 * hb + 2), slice(128, 768)).rg))
            else:
                rope_batch(ksrc, 128, 512, View(KT.h[0:32, 2 * hb:2 * hb + 2, 256:768],
                                                KT(A, slice(2 * hb, 2 * hb + 2), slice(256, 768)).rg))

        p = next_set()
        xb_ = 4 * (1 - p)
        vtiles = [(1, 4 * p), (2, 4 * p + 1), (3, 4 * p + 2), (4, 4 * p + 3)] + ([(0, xb_)] if first_tile else [])
        gemm_A(hT, vtiles, 4)
        P.op("act", lambda e, p=p: e.activation(out=VA.h[:, 2:6, :], in_=ps.h[:, 4 * p:4 * p + 4, :],
                                                func=AF.Copy),
             [ps.f(A, slice(4 * p, 4 * p + 4), 0, 512)], [VA(A, slice(2, 6))])
        if first_tile:
            P.op("act", lambda e, xb_=xb_: e.activation(out=VA.h[:, 1, :], in_=ps.h[:, xb_, :], func=AF.Copy),
                 [ps.f(A, xb_, 0, 512)], [VA(A, 1)])

        dve_bg_flush(NFLUSH)
        for j in range(4):
            p = next_set()
            gemm_B(hT, [(0, 512, 0)], 4, lambda jj, pi, p=p: (4 * p + jj, 0))
            for (pa, pb) in ((32, 64), (64, 128)):
                P.op("act", lambda e, p=p, pa=pa, pb=pb, j=j: e.activation(
                    out=QT.h[pa:pb, 4 * j:4 * j + 4, :], in_=ps.h[pa:pb, 4 * p:4 * p + 4, :], func=AF.Copy),
                    [ps.f(A, slice(4 * p, 4 * p + 4), 0, 512)], [QT(A, slice(4 * j, 4 * j + 4))])
            for hb in range(2):
                def qsrc(rv, p=p, hb=hb):
                    P.op("act", lambda e: e.activation(
                        out=rraw.h[0:32, :, 0:512], in_=ps.h[0:32, 4 * p + 2 * hb:4 * p + 2 * hb + 2, :],
                        func=AF.Copy),
                        [ps.f(A, slice(4 * p + 2 * hb, 4 * p + 2 * hb + 2), 0, 512)], [rraw(A, A, slice(0, 512))])
                h0 = 4 * j + 2 * hb
                rope_batch(qsrc, 0, 512, View(QT.h[0:32, h0:h0 + 2, :], QT(A, slice(h0, h0 + 2)).rg))
        if stop_after == "inproj" and ti == ntiles - 1:
            bg_flush()
            dve_bg_flush()
            dump("QT", QT)
            dump("KT", KT)
            dump("VA", VA)
            dump("uT", uT)
            stopped = True
            break

        bg_flush()
        dve_bg_flush()
        S1 = ps.f(A, 0, 0, 512)
        S2 = ps.f(A, 1, 0, 512)
        sqs = [lnt[0], acc[0]]
        for c in range(16):
            yv = uT(A, c, slice(0, 512))
            sq = sqs[c % 2]
            P.op("pe", lambda e, c=c: e.matmul(ps.h[:, 0, :], lhsT=ones.h[:, :], rhs=uT.h[:, c, 0:512],
                                               start=(c == 0), stop=(c == 15)), [ones(), yv], [S1])
            P.op("act", lambda e, c=c, sq=sq: e.activation(out=sq().ap, in_=uT.h[:, c, 0:512], func=AF.Square),
                 [yv], [sq()])
            P.op("pe", lambda e, c=c, sq=sq: e.matmul(ps.h[:, 1, :], lhsT=ones.h[:, :], rhs=sq.h[:, :],
                                                      start=(c == 0), stop=(c == 15)), [ones(), sq()], [S2])
        mean, msq, var, rstd_ln = lnt[1], lnt[2], lnt[3], lnt[4]
        P.op("act", lambda e: e.mul(out=mean().ap, in_=ps.h[:, 0, :], mul=1.0 / 2048),
             [S1], [mean()])
        P.op("dve", lambda e: e.tensor_tensor(out=msq().ap, in0=mean().ap, in1=mean().ap, op=ALU.mult),
             [mean()], [msq()])
        P.op("dve", lambda e: e.scalar_tensor_tensor(out=var().ap, in0=ps.h[:, 1, :], scalar=1.0 / 2048,
                                                     in1=msq().ap, op0=ALU.mult, op1=ALU.subtract),
             [S2, msq()], [var()])
        P.op("dve", lambda e: e.tensor_scalar(out=var().ap, in0=var().ap, scalar1=LN_EPS, scalar2=None,
                                              op0=ALU.add), [var()], [var()])
        P.op("act", lambda e: e.activation(out=var().ap, in_=var().ap, func=AF.Sqrt), [var()], [var()])
        P.op("dve", lambda e: e.reciprocal(out=rstd_ln().ap, in_=var().ap), [var()], [rstd_ln()])
        S3 = ps.f(A, 2, 0, 512)
        for c in range(16):
            yv = uT(A, c, slice(0, 512))
            P.op("dve", lambda e, c=c: e.tensor_tensor(out=uT.h[:, c, 0:512], in0=uT.h[:, c, 0:512],
                                                       in1=mean().ap, op=ALU.subtract), [yv, mean()], [yv])
            P.op("dve", lambda e, c=c: e.tensor_tensor(out=uT.h[:, c, 0:512], in0=uT.h[:, c, 0:512],
                                                       in1=rstd_ln().ap, op=ALU.mult), [yv, rstd_ln()], [yv])
            gv = cvec(A, 1, slice(c, c + 1))
            bv = cvec(A, 2, slice(c, c + 1))
            P.op("act", lambda e, c=c, gv=gv, bv=bv: e.activation(
                out=uT.h[:, c, 0:512], in_=uT.h[:, c, 0:512], func=AF.Silu, scale=gv.ap, bias=bv.ap),
                [yv, gv, bv], [yv])
            sq = sqs[c % 2]
            P.op("act", lambda e, c=c, sq=sq: e.activation(out=sq().ap, in_=uT.h[:, c, 0:512], func=AF.Square),
                 [yv], [sq()])
            P.op("pe", lambda e, c=c, sq=sq: e.matmul(ps.h[:, 2, :], lhsT=ones.h[:, :], rhs=sq.h[:, :],
                                                      start=(c == 0), stop=(c == 15)), [ones(), sq()], [S3])
        r3 = lnt[5]
        P.op("dve", lambda e: e.tensor_scalar(out=r3().ap, in0=ps.h[:, 2, :], scalar1=1.0 / 2048,
                                              scalar2=RMS_EPS, op0=ALU.mult, op1=ALU.add), [S3], [r3()])
        P.op("act", lambda e: e.activation(out=r3().ap, in_=r3().ap, func=AF.Sqrt), [r3()], [r3()])
        P.op("dve", lambda e: e.reciprocal(out=r3().ap, in_=r3().ap), [r3()], [r3()])
        for c in range(16):
            yv = uT(A, c, slice(0, 512))
            gv = cvec(A, 3, slice(c, c + 1))
            mv = mergedT(A, 16 + c)
            P.op("dve", lambda e, c=c, gv=gv, mv=mv: e.scalar_tensor_tensor(
                out=mv.ap, in0=uT.h[:, c, 0:512], scalar=gv.ap, in1=r3().ap, op0=ALU.mult, op1=ALU.mult),
                [yv, gv, r3()], [mv])

        P.dma("sp", "sa", lambda e: e.dma_start(out=attng().ap, in_=attn_g_d[0:1, :].partition_broadcast(128)),
              (), [attng()])
        groups = [(qb, kv) for qb in range(4) for kv in range(4)]

        def att_scores(gi):
            qb, kv = groups[gi]
            full = not (ti == 0 and qb == 0)
            kc0 = 128 * qb if full else 128
            nk = 384 if full else 256
            pc0 = 0 if full else 128
            for hh in range(4):
                h = 4 * kv + hh
                P.op("pe", lambda e, h=h, hh=hh: e.matmul(
                    ps.h[:, hh, pc0:pc0 + nk], lhsT=QT.h[:, h, 128 * qb:128 * qb + 128],
                    rhs=KT.h[:, kv, kc0:kc0 + nk], start=True, stop=True),
                    [QT(A, h, slice(128 * qb, 128 * qb + 128)), KT(A, kv, slice(kc0, kc0 + nk))],
                    [ps.f(A, hh, pc0, pc0 + nk)])

        def att_sm1(gi):
            qb, kv = groups[gi]
            full = not (ti == 0 and qb == 0)
            pc0 = 0 if full else 128
            nk = 384 - pc0
            b = gi % 2
            sb_ = s_sb[b]
            sv = sb_(A, A, slice(pc0, 384))
            P.op("dve", lambda e: e.scalar_tensor_tensor(
                out=sv.ap, in0=ps.h[:, 0:4, pc0:384], scalar=inv_sqrt_d,
                in1=maskb.h[:, pc0:384].unsqueeze(1).to_broadcast([128, 4, nk]),
                op0=ALU.mult, op1=ALU.add), [ps.f(A, slice(0, 4), pc0, 384), maskb()], [sv])
            mx = st2(A, b, slice(0, 4))
            P.op("dve", lambda e: e.tensor_reduce(out=mx.ap, in_=sv.ap, axis=AX.X, op=ALU.max), [sv], [mx])
            skv = sinkb(A, slice(4 * kv, 4 * kv + 4))
            P.op("dve", lambda e: e.tensor_tensor(out=mx.ap, in0=mx.ap, in1=skv.ap, op=ALU.max), [mx, skv], [mx])
            nmx = st2(A, b, slice(4, 8))
            P.op("dve", lambda e: e.tensor_scalar(out=nmx.ap, in0=mx.ap, scalar1=-1.0, scalar2=None,
                                                  op0=ALU.mult), [mx], [nmx])
            dsk = st2(A, b, slice(8, 12))
            P.op("dve", lambda e: e.tensor_tensor(out=dsk.ap, in0=skv.ap, in1=mx.ap, op=ALU.subtract),
                 [skv, mx], [dsk])

        def att_exps(gi):
            qb, kv = groups[gi]
            full = not (ti == 0 and qb == 0)
            pc0 = 0 if full else 128
            b = gi % 2
            sb_ = s_sb[b]
            pb_ = Pb[b]
            sv = sb_(A, A, slice(pc0, 384))
            nmx = st2(A, b, slice(4, 8))
            dsk = st2(A, b, slice(8, 12))
            rsum = st2(A, b, slice(12, 16))
            for hh in range(4):
                P.op("act", lambda e, hh=hh: e.activation(
                    out=pb_.h[:, hh, pc0:384], in_=sb_.h[:, hh, pc0:384], func=AF.Exp,
                    bias=st2.h[:, b, 4 + hh:5 + hh], accum_out=st2.h[:, b, 12 + hh:13 + hh]),
                    [sv, nmx], [pb_(A, hh), rsum])
            esk = st2(A, b, slice(16, 20))
            P.op("act", lambda e: e.activation(out=esk.ap, in_=dsk.ap, func=AF.Exp), [dsk], [esk])

        def att_sm2(gi):
            b = gi % 2
            rsum = st2(A, b, slice(12, 16))
            esk = st2(A, b, slice(16, 20))
            den = st2(A, b, slice(20, 24))
            P.op("dve", lambda e: e.tensor_tensor(out=den.ap, in0=rsum.ap, in1=esk.ap, op=ALU.add),
                 [rsum, esk], [den])
            rden = st2(A, b, slice(24, 28))
            P.op("dve", lambda e: e.reciprocal(out=rden.ap, in_=den.ap), [den], [rden])

        def att_pv(gi):
            qb, kv = groups[gi]
            full = not (ti == 0 and qb == 0)
            j0 = 0 if full else 1
            pb_ = Pb[gi % 2]
            pt_ = PTs[gi % 2]
            for half in range(2):
                bank = 4 + half
                pv_ = ps.b(A, bank, 0, 768)

                def tr(e, half=half, bank=bank):
                    ins = None
                    for hh2 in range(2):
                        hh = 2 * half + hh2
                        for j in range(j0, 3):
                            ins = e.transpose(ps.hb[:, bank, hh2 * 384 + j * 128: hh2 * 384 + (j + 1) * 128],
                                              pb_.h[:, hh, j * 128:(j + 1) * 128], ident.h[:, :])
                    return ins
                P.op("pe", tr, [pb_(A, slice(2 * half, 2 * half + 2)), ident()], [pv_])
                c_lo = 128 * j0
                P.op("act", lambda e, half=half, bank=bank, c_lo=c_lo: e.activation(
                    out=pt_.h[:, 2 * half:2 * half + 2, c_lo:384],
                    in_=ps.hb[:, bank, 0:768].rearrange("p (a b) -> p a b", a=2)[:, :, c_lo:384], func=AF.Copy),
                    [pv_], [pt_(A, slice(2 * half, 2 * half + 2))])
            ov = ps.f(A, 6, 0, 512)

            def pvmm(e):
                ins = None
                for hh in range(4):
                    for j in range(j0, 3):
                        ins = e.matmul(ps.h[:, 6, hh * 128:(hh + 1) * 128],
                                       lhsT=pt_.h[:, hh, j * 128:(j + 1) * 128],
                                       rhs=VA.h[:, qb + j, kv * 128:(kv + 1) * 128],
                                       start=(j == j0), stop=(j == 2))
                return ins
            P.op("pe", pvmm, [pt_(), VA(A, slice(qb, qb + 3))], [ov])
            rden = st2(A, gi % 2, slice(24, 28))
            av = atok(A, slice(512 * kv, 512 * kv + 512))
            P.op("dve", lambda e: e.tensor_tensor(
                out=av.ap.rearrange("p (a b) -> p a b", a=4),
                in0=ps.h[:, 6, :].rearrange("p (a b) -> p a b", a=4),
                in1=rden.ap.unsqueeze(2).to_broadcast([128, 4, 128]), op=ALU.mult), [ov, rden], [av])

        def att_finish(qb):
            ssv = st(A, slice(40, 41))
            P.op("act", lambda e: e.activation(out=anb().ap, in_=atok().ap, func=AF.Square, accum_out=ssv.ap),
                 [atok()], [anb(), ssv])
            rs = st(A, slice(41, 42))
            rstd_op(ssv, 2048, RMS_EPS, rs, 42)
            P.op("dve", lambda e: e.scalar_tensor_tensor(out=anb().ap, in0=atok().ap, scalar=rs.ap,
                                                         in1=attng().ap, op0=ALU.mult, op1=ALU.mult),
                 [atok(), rs, attng()], [anb()])
            for half in range(2):
                pv_ = ps.b(A, 7, 0, 1024)

                def tr(e, half=half):
                    ins = None
                    for i in range(8):
                        c = half * 8 + i
                        ins = e.transpose(ps.hb[:, 7, i * 128:(i + 1) * 128],
                                          anb.h[:, c * 128:(c + 1) * 128], ident.h[:, :])
                    return ins
                P.op("pe", tr, [anb(), ident()], [pv_])
                mv = mergedT(A, slice(half * 8, half * 8 + 8), slice(128 * qb, 128 * qb + 128))
                P.op("act", lambda e, mv=mv: e.activation(
                    out=mv.ap, in_=ps.hb[:, 7, :].rearrange("p (a b) -> p a b", a=8), func=AF.Copy),
                    [pv_], [mv])

        ng = len(groups)
        att_scores(0)
        att_sm1(0)
        for gi in range(ng):
            if gi + 1 < ng:
                att_scores(gi + 1)
            att_exps(gi)
            if gi + 1 < ng:
                att_sm1(gi + 1)
            att_sm2(gi)
            att_pv(gi)
            if groups[gi][1] == 3:
                att_finish(groups[gi][0])
        if ti + 1 < ntiles:
            P.op("act", lambda e: e.activation(out=KT.h[:, :, 0:256], in_=KT.h[:, :, 512:768], func=AF.Copy),
                 [KT(A, A, slice(512, 768))], [KT(A, A, slice(0, 256))])
            P.op("act", lambda e: e.activation(out=VA.h[:, 0:2, :], in_=VA.h[:, 4:6, :], func=AF.Copy),
                 [VA(A, slice(4, 6))], [VA(A, slice(0, 2))])
        if stop_after == "merged" and ti == ntiles - 1:
            dump("mergedT", mergedT)
            stopped = True
            break

        for j in range(8):
            p = next_set()
            gemm_A(mergedT, [(tt, 4 * p + tt) for tt in range(4)], 4)
            rv = r_sb(A, A, slice(512 * j, 512 * j + 512))
            P.op("act", lambda e, p=p, rv=rv: e.activation(out=rv.ap, in_=ps.h[:, 4 * p:4 * p + 4, :], func=AF.Copy),
                 [ps.f(A, slice(4 * p, 4 * p + 4), 0, 512)], [rv])
        P.dma("sp", "sb1", lambda e: e.dma_start(out=gbc1().ap, in_=post1_d[0:1, :].partition_broadcast(128)),
              (), [gbc1()])
        def _oepi(tt, s0=s0):
            r0 = s0 + 128 * tt
            b = tt % 2
            xeb = xe if b == 0 else xe2
            xsv = xsr[tt]
            c0 = 44 + 8 * b
            P.dma("sp", "xe%d" % b, lambda e: e.dma_start(out=xeb().ap, in_=x_d[r0:r0 + 128, :]), (), [xeb()])
            rv = r_sb(A, tt)
            ssv = st(A, slice(c0, c0 + 1))
            P.op("act", lambda e: e.activation(out=xse().ap, in_=rv.ap, func=AF.Square, accum_out=ssv.ap),
                 [rv], [xse(), ssv])
            rs = st(A, slice(c0 + 1, c0 + 2))
            rstd_op(ssv, D, RMS_EPS, rs, c0 + 2)
            P.op("dve", lambda e: e.scalar_tensor_tensor(out=rv.ap, in0=rv.ap, scalar=rs.ap, in1=gbc1().ap,
                                                         op0=ALU.mult, op1=ALU.mult), [rv, rs, gbc1()], [rv])
            P.op("dve", lambda e: e.tensor_tensor(out=xeb().ap, in0=xeb().ap, in1=rv.ap, op=ALU.add),
                 [xeb(), rv], [xeb()])
            cn = "ys%d" % b
            if cn not in final_chans:
                final_chans.append(cn)
            P.dma("sp", cn, lambda e: e.dma_start(out=y_d[r0:r0 + 128, :], in_=xeb().ap),
                  [xeb()], [DR(r0 // 128)])
            ss2 = st(A, slice(c0 + 4, c0 + 5))
            P.op("act", lambda e: e.activation(out=xse().ap, in_=xeb().ap, func=AF.Square, accum_out=ss2.ap),
                 [xeb()], [xse(), ss2])
            rs2 = st(A, slice(c0 + 5, c0 + 6))
            rstd_op(ss2, D, RMS_EPS, rs2, c0 + 6)
            P.op("act", lambda e: e.activation(out=xsv().ap, in_=xeb().ap, func=AF.Copy, scale=rs2.ap),
                 [xeb(), rs2], [xsv()])
            to_featmajor(xsv, h2T, 128 * tt, g2T, tt % 2)
        for tt in range(4):
            _oepi(tt)
        if stop_after == "h2T" and ti == ntiles - 1:
            dump("h2T", h2T)
            stopped = True
            break

        for j in range(22):
            nj = 4 if j < 21 else 2
            sg = sgsb[j % 2]
            p = next_set()
            gemm_B(h2T, [(0, 512, 0)], 4, lambda jj, pi, p=p: (4 * p + jj, 0))
            P.op("act", lambda e, p=p, sg=sg, nj=nj: e.activation(
                out=sg.h[:, 0:nj, :], in_=ps.h[:, 4 * p:4 * p + nj, :], func=AF.Silu),
                [ps.f(A, slice(4 * p, 4 * p + nj), 0, 512)], [sg(A, slice(0, nj))])
            p = next_set()
            gemm_B(h2T, [(0, 512, 0)], 4, lambda jj, pi, p=p: (4 * p + jj, 0))
            av = actT(A, slice(4 * j, 4 * j + nj))
            P.op("dve", lambda e, p=p, sg=sg, nj=nj, av=av: e.tensor_tensor(
                out=av.ap, in0=ps.h[:, 4 * p:4 * p + nj, :], in1=sg.h[:, 0:nj, :], op=ALU.mult),
                [ps.f(A, slice(4 * p, 4 * p + nj), 0, 512), sg(A, slice(0, nj))], [av])
        if stop_after == "act" and ti == ntiles - 1:
            dump("actT", actT)
            stopped = True
            break

        for j in range(8):
            p = next_set()
            gemm_A(actT, [(tt, 4 * p + tt) for tt in range(4)], 11)
            fv = f_sb(A, A, slice(512 * j, 512 * j + 512))
            P.op("act", lambda e, p=p, fv=fv: e.activation(out=fv.ap, in_=ps.h[:, 4 * p:4 * p + 4, :], func=AF.Copy),
                 [ps.f(A, slice(4 * p, 4 * p + 4), 0, 512)], [fv])
        P.dma("sp", "sb2", lambda e: e.dma_start(out=gbc2().ap, in_=post2_d[0:1, :].partition_broadcast(128)),
              (), [gbc2()])
        def _fepi(tt, s0=s0):
            r0 = s0 + 128 * tt
            xb = xf[tt % 2]
            P.dma("sp", "xf%d" % (tt % 2), lambda e, r0=r0, xb=xb: e.dma_start(out=xb().ap, in_=y_d[r0:r0 + 128, :]),
                  [DR(r0 // 128)], [xb()])
            fv = f_sb(A, tt)
            ssv = st(A, slice(4, 5))
            junk = actT(A, slice(80, 84))
            P.op("act", lambda e, fv=fv: e.activation(out=junk.ap.rearrange("p a b -> p (a b)"), in_=fv.ap[:, 0:2048],
                                                      func=AF.Square, accum_out=ssv.ap), [fv], [junk, ssv])
            ssv2 = st(A, slice(5, 6))
            P.op("act", lambda e, fv=fv: e.activation(out=junk.ap.rearrange("p a b -> p (a b)"), in_=fv.ap[:, 2048:4096],
                                                      func=AF.Square, accum_out=ssv2.ap), [fv], [junk, ssv2])
            sst = st(A, slice(6, 7))
            P.op("dve", lambda e: e.tensor_tensor(out=sst.ap, in0=ssv.ap, in1=ssv2.ap, op=ALU.add), [ssv, ssv2], [sst])
            rs = st(A, slice(7, 8))
            rstd_op(sst, D, RMS_EPS, rs, 8)
            P.op("dve", lambda e, fv=fv: e.scalar_tensor_tensor(out=fv.ap, in0=fv.ap, scalar=rs.ap, in1=gbc2().ap,
                                                                op0=ALU.mult, op1=ALU.mult), [fv, rs, gbc2()], [fv])
            P.op("dve", lambda e, fv=fv, xb=xb: e.tensor_tensor(out=xb().ap, in0=xb().ap, in1=fv.ap, op=ALU.add),
                 [xb(), fv], [xb()])
            cn = "yo%d" % (tt % 2)
            if cn not in final_chans:
                final_chans.append(cn)
            P.dma("sp", cn, lambda e, r0=r0, xb=xb: e.dma_start(out=y_d[r0:r0 + 128, :], in_=xb().ap),
                  [xb()], [DR(r0 // 128)])
        for tt in range(4):
            _fepi(tt)

    if "dbg" in P.chan or stopped:
        final_chans.append("dbg")
    P.emit(final_chans)
    return nc, dbg


def _consts():
    ident = np.eye(128, dtype=np.float32).astype(ml_dtypes.bfloat16)
    q = np.arange(128)[:, None]
    j = np.arange(384)[None, :]
    valid = np.where(j < 128, j >= q, np.where(j < 256, True, (j - 256) <= q))
    maskb = np.where(valid, 0.0, NEG).astype(np.float32)
    invf = (500000.0 ** (-np.arange(0, 32, 2, dtype=np.float32) / np.float32(32))).astype(np.float32)
    ropec = np.zeros((128, 4), np.float32)
    ropec[0:16, 0] = invf
    ropec[16:32, 0] = invf
    ropec[0:16, 1] = -1.0
    ropec[16:32, 1] = 1.0
    return ident, maskb, ropec


def _core_maps(inp):
    x = np.asarray(inp["x"], dtype=np.float32)
    pos = np.asarray(inp["positions"]).astype(np.int32)
    ident, maskb, ropec = _consts()

    def gT(v):
        return np.ascontiguousarray(np.asarray(v, np.float32).reshape(-1, 128).T)

    common = {
        "w_in": np.asarray(inp["w_in"], np.float32)[0],
        "w_out": np.asarray(inp["w_out"], np.float32)[0],
        "w_gate": np.asarray(inp["w_gate"], np.float32)[0],
        "w_up": np.asarray(inp["w_up"], np.float32)[0],
        "w_down": np.asarray(inp["w_down"], np.float32)[0],
        "g1T": gT(inp["mix_pre_g"][0]),
        "g2T": gT(inp["ffn_pre_g"][0]),
        "cvec": np.ascontiguousarray(np.concatenate(
            [gT(inp["conv_b"][0]), gT(inp["conv_ln_g"][0]), gT(inp["conv_ln_b"][0]), gT(inp["conv_out_g"][0])],
            axis=1)),
        "sinks": np.asarray(inp["sinks"], np.float32).reshape(1, 16),
        "attn_g": np.asarray(inp["attn_out_g"], np.float32).reshape(1, 2048),
        "post1": np.asarray(inp["mix_post_g"], np.float32).reshape(1, D),
        "post2": np.asarray(inp["ffn_post_g"], np.float32).reshape(1, D),
        "ident": ident, "maskb": maskb, "ropec": ropec,
    }
    cw = np.asarray(inp["conv_w"], np.float32)[0]
    def cwdev(c):
        return np.ascontiguousarray(c.T.reshape(16, 128, 31).transpose(1, 0, 2).reshape(128, 16 * 31))
    cw_f = cwdev(cw)
    cw_r = cwdev(cw[::-1])
    maps = []
    idxs = []
    for c in range(8):
        b, half = c // 2, c % 2
        if half == 0:
            idx = np.arange(0, NLOC)
        else:
            idx = 2047 - np.arange(0, NLOC)
        idxs.append((b, idx))
        m = dict(common)
        m["x"] = np.ascontiguousarray(x[b, idx])
        m["pos"] = np.ascontiguousarray(pos[b, idx]).reshape(1, NLOC)
        m["convw"] = cw_f if half == 0 else cw_r
        maps.append(m)
    return maps, idxs


_NC_CACHE = {}


def kernel(**inputs):
    maps, idxs = _core_maps(inputs)
    if "nc" not in _NC_CACHE:
        _NC_CACHE["nc"] = build_program()[0]
    nc = _NC_CACHE["nc"]
    res = run_bass_kernel_spmd(nc, maps, core_ids=list(range(8)))
    out = np.empty((4, 2048, D), np.float32)
    for c in range(8):
        b, idx = idxs[c]
        out[b, idx[:NOWN]] = np.asarray(res.results[c]["y"], np.float32)
    return out
```

```python
import math
import numpy as np
import ml_dtypes
import concourse.bass as bass
import concourse.mybir as mybir
from concourse.bass_utils import run_bass_kernel_spmd

F32 = mybir.dt.float32
BF16 = mybir.dt.bfloat16
I32 = mybir.dt.int32
ALU = mybir.AluOpType
AF = mybir.ActivationFunctionType
AX = mybir.AxisListType

D = 4096
NCH = 32
TT = 512
TE = 640
NOWN = 1024
NLOC = 1152
DFF = 11008
NFC = 86
INW = 7168
RMS_EPS = 1e-6
LN_EPS = 1e-5
SB_BASE = 16640
SB_END = 229376
CELL = 256
NEG = -30000.0
ROPE_ENG = "dve"
N_POOL_CONV = 0
NFLUSH = 20
NPRE = 8


def _esize(dt):
    return 2 if dt == BF16 else 4


class View:
    __slots__ = ("ap", "rg")

    def __init__(self, ap, rg):
        self.ap = ap
        self.rg = rg


class SB:
    def __init__(self, nc, name, shape, dtype, off):
        assert off % 32 == 0 and off >= SB_BASE, (name, off)
        self.shape = list(shape)
        self.es = _esize(dtype)
        self.off = off
        fs = self.shape[1:]
        st = [1] * len(fs)
        for i in range(len(fs) - 2, -1, -1):
            st[i] = st[i + 1] * fs[i + 1]
        self.st = st
        self.nbytes = self.es * int(np.prod(fs))
        assert off + self.nbytes <= SB_END, (name, off, self.nbytes)
        self.h = nc.alloc_sbuf_tensor_at(name, self.shape, dtype, offset=off)
        self.space = "sb"

    def __call__(self, p=slice(None), *f):
        f = list(f) + [slice(None)] * (len(self.shape) - 1 - len(f))
        lo = 0
        hi = 0
        for i, s in enumerate(f):
            n = self.shape[1 + i]
            if isinstance(s, int):
                a, b = s, s + 1
            else:
                a = 0 if s.start is None else s.start
                b = n if s.stop is None else s.stop
            assert 0 <= a < b <= n, (s, n)
            lo += a * self.st[i]
            hi += (b - 1) * self.st[i]
        hi += 1
        ap = self.h[(p,) + tuple(f)]
        return View(ap, (self.space, self.off + lo * self.es, self.off + hi * self.es))


class PS:
    def __init__(self, nc):
        self.h = nc.alloc_psum_tensor("ps", [128, 8, 512], F32)
        self.hb = self.h.bitcast(BF16)

    def f(self, p, b, c0, c1):
        if isinstance(b, int):
            b0, b1 = b, b + 1
        else:
            b0, b1 = b.start, b.stop
        return View(self.h[p, b, c0:c1], ("ps", b0 * 2048 + c0 * 4, (b1 - 1) * 2048 + c1 * 4))

    def b(self, p, b, c0, c1):
        if isinstance(b, int):
            b0, b1 = b, b + 1
        else:
            b0, b1 = b.start, b.stop
        return View(self.hb[p, b, c0:c1], ("ps", b0 * 2048 + c0 * 2, (b1 - 1) * 2048 + c1 * 2))


class Prog:
    ENG = ("pe", "act", "dve", "pool", "sp")

    def __init__(self, nc):
        self.nc = nc
        self.q = {e: [] for e in self.ENG}
        self.esem = {e: nc.alloc_semaphore("s_" + e) for e in ("pe", "act", "dve", "pool")}
        self.ecnt = {e: 0 for e in self.esem}
        self.waited = {e: {} for e in self.ENG}
        ncell = (SB_END + CELL - 1) // CELL
        self.cw = {"sb": [None] * ncell, "ps": [None] * 64, "dr": {}}
        self.cr = {"sb": [None] * ncell, "ps": [None] * 64, "dr": {}}
        self.chan = {}
        self.nops = 0

    def _cells(self, rg):
        sp, lo, hi = rg
        if sp == "dr":
            return sp, range(lo, hi)
        return sp, range(lo // CELL, (hi - 1) // CELL + 1)

    def _need(self, reads, writes):
        need = {}

        def add(d):
            if d:
                for k, hv in d.items():
                    o = need.get(k)
                    if o is None or o[1] < hv[1]:
                        need[k] = hv
        for rg in reads:
            sp, cells = self._cells(rg)
            cw = self.cw[sp]
            for c in cells:
                add(cw[c] if sp != "dr" else cw.get(c))
        for rg in writes:
            sp, cells = self._cells(rg)
            cw = self.cw[sp]
            cr = self.cr[sp]
            for c in cells:
                if sp != "dr":
                    add(cw[c])
                    add(cr[c])
                else:
                    add(cw.get(c))
                    add(cr.get(c))
        return need

    def _commit(self, key, hv, reads, writes):
        for rg in writes:
            sp, cells = self._cells(rg)
            cw = self.cw[sp]
            for c in cells:
                d = cw[c] if sp != "dr" else cw.get(c)
                if d is None:
                    d = {}
                    cw[c] = d
                d[key] = hv
        for rg in reads:
            sp, cells = self._cells(rg)
            cr = self.cr[sp]
            for c in cells:
                d = cr[c] if sp != "dr" else cr.get(c)
                if d is None:
                    d = {}
                    cr[c] = d
                d[key] = hv

    def _waits(self, eng, need):
        waits = []
        wd = self.waited[eng]
        for k, (h, v) in need.items():
            if eng == "pe" and k == "pe":
                continue
            if wd.get(k, 0) >= v:
                continue
            wd[k] = v
            waits.append((h, v))
        return waits

    @staticmethod
    def _rgs(vs):
        return [v.rg if isinstance(v, View) else v for v in vs]

    def op(self, eng, fn, reads=(), writes=()):
        reads = self._rgs(reads)
        writes = self._rgs(writes)
        waits = self._waits(eng, self._need(reads, writes))
        self.ecnt[eng] += 1
        hv = (self.esem[eng], self.ecnt[eng])
        self.q[eng].append((waits, fn, self.esem[eng], 1))
        self._commit(eng, hv, reads, writes)
        self.nops += 1

    def dma(self, queue, chan, fn, reads=(), writes=()):
        reads = self._rgs(reads)
        writes = self._rgs(writes)
        if chan not in self.chan:
            self.chan[chan] = [self.nc.alloc_semaphore("c_" + chan), 0]
        c = self.chan[chan]
        waits = self._waits(queue, self._need(reads, writes))
        c[1] += 16
        hv = (c[0], c[1])
        self.q[queue].append((waits, fn, c[0], 16))
        self._commit("c_" + chan, hv, reads, writes)
        self.nops += 1

    def emit(self, final_chans):
        nc = self.nc
        q = self.q
        chan = self.chan

        def run(e, name):
            for waits, fn, sem, inc in q[name]:
                for h, v in waits:
                    e.wait_ge(h, v)
                ins = fn(e)
                ins.then_inc(sem, inc)
            if name == "sp":
                for cn in final_chans:
                    if cn in chan:
                        e.wait_ge(chan[cn][0], chan[cn][1])

        with nc.Block() as block:
            @block.tensor
            def _(e):
                run(e, "pe")

            @block.scalar
            def _(e):
                run(e, "act")

            @block.vector
            def _(e):
                run(e, "dve")

            @block.gpsimd
            def _(e):
                run(e, "pool")

            @block.sync
            def _(e):
                run(e, "sp")


def DR(key):
    return ("dr", key, key + 1)


def build_program(stop_after=None, ntiles=2):
    nc = bass.Bass("TRN2", target_bir_lowering=False)
    P = Prog(nc)
    dbg = {}

    def din(name, shape, dt=F32):
        return nc.dram_tensor(name, list(shape), dt, kind="ExternalInput")

    x_d = din("x", [NLOC, D])
    pos_d = din("pos", [1, NLOC], I32)
    w_in_d = din("w_in", [D, INW])
    w_out_d = din("w_out", [D, D])
    w_gate_d = din("w_gate", [D, DFF])
    w_up_d = din("w_up", [D, DFF])
    w_down_d = din("w_down", [DFF, D])
    g1T_d = din("g1T", [128, NCH])
    g2T_d = din("g2T", [128, NCH])
    cvec_d = din("cvec", [128, 64])
    convw_d = din("convw", [128, 16 * 31])
    sinks_d = din("sinks", [1, 16])
    attn_g_d = din("attn_g", [1, 2048])
    post1_d = din("post1", [1, D])
    post2_d = din("post2", [1, D])
    ident_d = din("ident", [128, 128], BF16)
    maskb_d = din("maskb", [128, 384])
    ropec_d = din("ropec", [128, 4])
    y_d = nc.dram_tensor("y", [NOWN, D], F32, kind="ExternalOutput")

    o = SB_BASE
    def alloc(name, shape, dt, off=None):
        nonlocal o
        if off is None:
            off = o
            t = SB(nc, name, shape, dt, off)
            o = (off + t.nbytes + 255) // 256 * 256
            return t
        return SB(nc, name, shape, dt, off)

    ident = alloc("ident", [128, 128], BF16)
    ones = alloc("ones", [128, 128], F32)
    maskb = alloc("maskb", [128, 384], F32)
    convw = alloc("convw", [128, 16, 31], F32)
    g1T = alloc("g1T", [128, NCH], F32)
    g2T = alloc("g2T", [128, NCH], F32)
    cvec = alloc("cvec", [128, 4, 16], F32)
    sinkb = alloc("sinkb", [128, 16], F32)
    ropec = alloc("ropec", [128, 4], F32)
    KT = alloc("KT", [128, 4, 768], BF16)
    VA = alloc("VA", [128, 6, 512], BF16)
    uprev = alloc("uprev", [128, 16, 32], F32)
    st = alloc("st", [128, 64], F32)
    st2 = alloc("st2", [128, 2, 64], F32)
    W0 = o
    NS = 4
    wslot = [alloc("w%d" % i, [128, 8, 512], BF16) for i in range(NS)]
    R2 = o
    o = R2 + 65536
    R1 = o
    R1SZ = SB_END - R1
    assert R1SZ >= 88064 + 3072, R1SZ

    hT = alloc("hT", [128, NCH, TE], BF16, R2)
    mergedT = alloc("mergedT", [128, NCH, TT], BF16, R2)
    h2T = alloc("h2T", [128, NCH, TT], BF16, R2)
    f_sb = alloc("f_sb", [128, 4, D], F32, R2)
    QT = alloc("QT", [128, 16, TT], BF16, R2 + 40960)
    attng = alloc("attng", [128, 2048], F32, R2 + 57344)
    gbc1 = alloc("gbc1", [128, D], F32, R2 + 32768)
    sgsb = [alloc("sg%d" % i, [128, 4, 512], F32, R2 + 32768 + 8192 * i) for i in range(2)]
    xt = [alloc("xt%d" % i, [128, D], F32, R1 + 16384 * i) for i in range(2)]
    xs = alloc("xs", [128, D], BF16, R1 + 32768)
    TB = R1 + 40960
    posi = alloc("posi", [32, TE], I32, TB)
    posf = alloc("posf", [32, TE], F32, TB + 2560)
    ang = alloc("ang", [32, TE], F32, TB + 5120)
    Ct = alloc("Ct", [32, TE], F32, TB + 7680)
    St = alloc("St", [32, TE], F32, TB + 10240)
    tki = alloc("tki", [32, TE], I32, R1 + 54272)
    tkf = alloc("tkf", [32, TE], F32, R1 + 54272 + 2560)
    tm = alloc("tm", [32, TE], F32, R1 + 54272 + 5120)
    uT = alloc("uT", [128, 16, 544], F32, R1)
    acc = [alloc("acc%d" % i, [128, 512], F32, R1 + 36864 + 2048 * i) for i in range(2)]
    accp = alloc("accp", [128, 512], F32, R1 + 34816)
    tmpp = alloc("tmpp", [128, 512], F32, R1 + 36864)
    SC = R1 + 54272
    sig = [alloc("sig%d" % i, [128, 4, 528], F32, SC + 8448 * i) for i in range(2)]
    cab = [alloc("cab%d" % i, [128, 4, 528], F32, SC + 16896 + 8448 * i) for i in range(2)]
    rraw = alloc("rraw", [32, 2, TE], F32, R2 + 57344)
    rsw = alloc("rsw", [32, 2, TE], F32, R1 + 88064)
    LS = R1 + 40960
    lnt = [alloc("lnt%d" % i, [128, 512], F32, LS + 2048 * i) for i in range(6)]
    s_sb = [alloc("s_sb%d" % i, [128, 4, 384], F32, SC + 6144 * i) for i in range(2)]
    Pb = [alloc("Pb%d" % i, [128, 4, 384], BF16, SC + 12288 + 3072 * i) for i in range(2)]
    PTs = [alloc("PTs%d" % i, [128, 4, 384], BF16, SC + 18432 + 3072 * i) for i in range(2)]
    atok = alloc("atok", [128, 2048], F32, SC + 24576)
    anb = alloc("anb", [128, 2048], BF16, SC + 32768)
    assert SC + 36864 <= SB_END
    r_sb = alloc("r_sb", [128, 4, D], F32, R1)
    xe = alloc("xe", [128, D], F32, R1 + 65536)
    xe2 = alloc("xe2", [128, D], F32, R2 + 49152)
    xsr = [alloc("xsr%d" % i, [128, D], BF16, R1 + 16384 * i) for i in range(4)]
    xse = alloc("xse", [128, D], BF16, R1 + 81920)
    actT = alloc("actT", [128, NFC, TT], BF16, R1)
    gbc2 = alloc("gbc2", [128, D], F32, R1)
    xf = [alloc("xf%d" % i, [128, D], F32, R1 + 16384 * (1 + i)) for i in range(2)]

    ps = PS(nc)
    A = slice(None)

    wq = []

    def wblocks(wd, K, f0, FW):
        nk = K // 128
        r = wd.rearrange("(kc p) f -> p kc f", p=128)
        out = []
        k0 = 0
        while k0 < nk:
            kc = min(8, nk - k0)
            out.append((r[:, k0:k0 + kc, f0:f0 + FW], kc, FW, k0))
            k0 += kc
        return out

    plan = []
    for ti in range(ntiles):
        gl = []
        for j in range(4):
            gl.append(("cg%d" % j, wblocks(w_in_d, D, 5120 + 512 * j, 512)))
            gl.append(("ca%d" % j, wblocks(w_in_d, D, 3072 + 512 * j, 512)))
        gl.append(("k", wblocks(w_in_d, D, 2048, 512)))
        gl.append(("v", wblocks(w_in_d, D, 2560, 512)))
        for j in range(4):
            gl.append(("q%d" % j, wblocks(w_in_d, D, 512 * j, 512)))
        for j in range(8):
            gl.append(("o%d" % j, wblocks(w_out_d, D, 512 * j, 512)))
        for j in range(22):
            fw = 512 if j < 21 else 256
            gl.append(("g%d" % j, wblocks(w_gate_d, D, 512 * j, fw)))
            gl.append(("u%d" % j, wblocks(w_up_d, D, 512 * j, fw)))
        for j in range(8):
            gl.append(("d%d" % j, wblocks(w_down_d, DFF, 512 * j, 512)))
        plan.append(gl)
    allblocks = []
    for gl in plan:
        for name, bl in gl:
            allblocks.extend(bl)
    wstate = {"issued": 0, "used": 0}
    pool_bg = []
    NBG = 6

    def bg_flush(n=None):
        k = len(pool_bg) if n is None else min(n, len(pool_bg))
        for _ in range(k):
            pool_bg.pop(0)()

    def w_issue():
        _w_issue()
        bg_flush(NBG)

    def _w_issue():
        i = wstate["issued"]
        if i >= len(allblocks):
            return
        apd, kc, fw, k0 = allblocks[i]
        sl = wslot[i % NS]
        dst = sl(A, slice(0, kc), slice(0, fw))
        P.dma("pool", "w%d" % (i % NS),
              lambda e, a=apd, d=dst.ap: e.dma_start(out=d, in_=a),
              reads=(), writes=(dst,))
        wstate["issued"] += 1

    def w_next():
        i = wstate["used"]
        apd, kc, fw, k0 = allblocks[i]
        wstate["used"] += 1
        return wslot[i % NS], kc, fw, k0

    def rstd_op(ss_v, n, eps, out_v, tmpcol):
        t1 = st(A, slice(tmpcol, tmpcol + 1))
        t2 = st(A, slice(tmpcol + 1, tmpcol + 2))
        P.op("dve", lambda e: e.tensor_scalar(out=t1.ap, in0=ss_v.ap, scalar1=1.0 / n, scalar2=eps,
                                              op0=ALU.mult, op1=ALU.add), [ss_v], [t1])
        P.op("act", lambda e: e.activation(out=t2.ap, in_=t1.ap, func=AF.Sqrt), [t1], [t2])
        P.op("dve", lambda e: e.reciprocal(out=out_v.ap, in_=t2.ap), [t2], [out_v])

    def to_featmajor(src_bf, dstT, col0, gT, bankset):
        for b4 in range(4):
            bank = bankset * 4 + b4
            pv = ps.b(A, bank, 0, 1024)

            def tr(e, b4=b4, bank=bank):
                ins = None
                for i in range(8):
                    c = b4 * 8 + i
                    ins = e.transpose(ps.hb[:, bank, i * 128:(i + 1) * 128],
                                      src_bf.h[:, c * 128:(c + 1) * 128], ident.h[:, :])
                return ins
            P.op("pe", tr, [src_bf(A, slice(b4 * 1024, b4 * 1024 + 1024)), ident()], [pv])
            dv = dstT(A, slice(b4 * 8, b4 * 8 + 8), slice(col0, col0 + 128))
            gv = gT(A, slice(b4 * 8, b4 * 8 + 8))
            P.op("dve", lambda e, bank=bank, dv=dv, gv=gv: e.tensor_tensor(
                out=dv.ap, in0=ps.hb[:, bank, :].rearrange("p (a b) -> p a b", a=8),
                in1=gv.ap.unsqueeze(2).to_broadcast([128, 8, 128]), op=ALU.mult),
                [pv, gv], [dv])

    def gemm_B(rhsT, pieces, nblk_expected, bankfn):
        nb = nblk_expected
        for bi in range(nb):
            sl, kc, fw, k0 = w_next()
            nj = fw // 128
            first = bi == 0
            last = bi == nb - 1
            reads = [sl(A, slice(0, kc), slice(0, fw)), rhsT(A, slice(k0, k0 + kc))]
            writes = []
            for j in range(nj):
                for (a, n, pi) in pieces:
                    bank, col = bankfn(j, pi)
                    writes.append(ps.f(A, bank, col, col + n))

            def mm(e, sl=sl, kc=kc, nj=nj, k0=k0, first=first, last=last):
                ins = None
                for k in range(kc):
                    for j in range(nj):
                        for (a, n, pi) in pieces:
                            bank, col = bankfn(j, pi)
                            if pi == 0:
                                ins = e.matmul(ps.h[:, bank, col:col + n],
                                               lhsT=sl.h[:, k, j * 128:(j + 1) * 128],
                                               rhs=rhsT.h[:, k0 + k, a:a + n],
                                               start=(first and k == 0), stop=(last and k == kc - 1))
                            else:
                                ins = e.matmul(ps.h[:, bank, col:col + n],
                                               lhsT=sl.h[:, k, j * 128:(j + 1) * 128],
                                               rhs=rhsT.h[:, k0 + k, a:a + n],
                                               start=(first and k == 0 and j == 0),
                                               stop=(last and k == kc - 1 and j == nj - 1),
                                               skip_group_check=True)
                return ins
            P.op("pe", mm, reads, writes)
            w_issue()

    def gemm_A(lhsT_src, tiles, nblk, fw=512):
        for bi in range(nblk):
            sl, kc, fw_, k0 = w_next()
            assert fw_ == fw
            first = bi == 0
            last = bi == nblk - 1
            reads = [sl(A, slice(0, kc), slice(0, fw)), lhsT_src(A, slice(k0, k0 + kc))]
            writes = [ps.f(A, bank, 0, fw) for (tt, bank) in tiles]

            def mm(e, sl=sl, kc=kc, k0=k0, first=first, last=last):
                ins = None
                for k in range(kc):
                    for (tt, bank) in tiles:
                        ins = e.matmul(ps.h[:, bank, 0:fw],
                                       lhsT=lhsT_src.h[:, k0 + k, tt * 128:(tt + 1) * 128],
                                       rhs=sl.h[:, k, 0:fw],
                                       start=(first and k == 0), stop=(last and k == kc - 1))
                return ins
            P.op("pe", mm, reads, writes)
            w_issue()

    def dump(name, sbt, dt=None):
        shape = sbt.shape
        dd = nc.dram_tensor("dbg_" + name, shape, BF16 if sbt.es == 2 else F32, kind="ExternalOutput")
        dbg[name] = dd
        P.dma("sp", "dbg", lambda e: e.dma_start(out=dd[tuple([A] * len(shape))], in_=sbt().ap),
              reads=[sbt()], writes=())

    _ldn = [0]

    def ld(dst, src_ap):
        _ldn[0] += 1
        P.dma("sp", "su%d" % _ldn[0], lambda e: e.dma_start(out=dst.ap, in_=src_ap), (), [dst])

    ld(ident(), ident_d[:, :])
    ld(maskb(), maskb_d[:, :])
    ld(convw(), convw_d.rearrange("p (c k) -> p c k", c=16))
    ld(g1T(), g1T_d[:, :])
    ld(g2T(), g2T_d[:, :])
    ld(cvec(), cvec_d.rearrange("p (a c) -> p a c", a=4))
    ld(sinkb(), sinks_d[0:1, :].partition_broadcast(128))
    ld(ropec(), ropec_d[:, :])
    P.op("dve", lambda e: e.memset(ones().ap, 1.0), (), [ones()])
    P.op("dve", lambda e: e.memset(uprev().ap, 0.0), (), [uprev()])
    P.op("dve", lambda e: e.memset(KT().ap, 0.0), (), [KT()])
    P.op("dve", lambda e: e.memset(VA().ap, 0.0), (), [VA()])
    for _ in range(NS):
        _w_issue()

    inv_sqrt_d = 1.0 / math.sqrt(128.0)
    TWO_PI = 2.0 * math.pi
    final_chans = []
    stopped = False

    for ti in range(ntiles):
        s0 = ti * TT
        gset = [0]

        def next_set():
            p = gset[0] % 2
            gset[0] += 1
            return p

        P.dma("sp", "pos", lambda e, s0=s0: e.dma_start(
            out=posi().ap, in_=pos_d[0:1, s0:s0 + TE].partition_broadcast(32)), (), [posi()])
        P.op("dve", lambda e: e.tensor_copy(out=posf().ap, in_=posi().ap), [posi()], [posf()])
        rc_f = ropec(slice(0, 32), slice(0, 1))
        rc_s = ropec(slice(0, 32), slice(1, 2))
        def sin_table(dst, phase, scale_ap, extra_reads):
            P.op("dve", lambda e: e.tensor_scalar(out=ang().ap, in0=posf().ap, scalar1=rc_f.ap,
                                                  scalar2=phase, op0=ALU.mult, op1=ALU.add),
                 [posf(), rc_f], [ang()])
            P.op("dve", lambda e: e.tensor_scalar(out=tki().ap, in0=ang().ap, scalar1=1.0 / TWO_PI,
                                                  scalar2=None, op0=ALU.mult), [ang()], [tki()])
            P.op("dve", lambda e: e.tensor_copy(out=tkf().ap, in_=tki().ap), [tki()], [tkf()])
            P.op("dve", lambda e: e.scalar_tensor_tensor(out=ang().ap, in0=tkf().ap, scalar=-TWO_PI,
                                                         in1=ang().ap, op0=ALU.mult, op1=ALU.add),
                 [tkf(), ang()], [ang()])
            P.op("dve", lambda e: e.tensor_scalar(out=tm().ap, in0=ang().ap, scalar1=math.pi,
                                                  scalar2=-TWO_PI, op0=ALU.is_gt, op1=ALU.mult),
                 [ang()], [tm()])
            P.op("dve", lambda e: e.tensor_tensor(out=ang().ap, in0=ang().ap, in1=tm().ap, op=ALU.add),
                 [ang(), tm()], [ang()])
            P.op("dve", lambda e: e.tensor_scalar(out=tm().ap, in0=ang().ap, scalar1=-math.pi,
                                                  scalar2=TWO_PI, op0=ALU.is_lt, op1=ALU.mult),
                 [ang()], [tm()])
            P.op("dve", lambda e: e.tensor_tensor(out=ang().ap, in0=ang().ap, in1=tm().ap, op=ALU.add),
                 [ang(), tm()], [ang()])
            P.op("dve", lambda e: e.tensor_scalar(out=ang().ap, in0=ang().ap, scalar1=-math.pi,
                                                  scalar2=math.pi, op0=ALU.max, op1=ALU.min),
                 [ang()], [ang()])
            if scale_ap is None:
                P.op("act", lambda e: e.activation(out=dst().ap, in_=ang().ap, func=AF.Sin), [ang()], [dst()])
            else:
                P.op("act", lambda e: e.activation(out=dst().ap, in_=ang().ap, func=AF.Sin, scale=scale_ap),
                     [ang()] + extra_reads, [dst()])

        sin_table(Ct, 0.5 * math.pi, None, [])
        sin_table(St, 0.0, rc_s.ap, [rc_s])

        def _xtile(m):
            xb = xt[m % 2]
            r0 = s0 + 128 * m
            cb0 = 0 if m % 2 == 0 else 12
            P.dma("sp", "xl%d" % (m % 2), lambda e, xb=xb, r0=r0: e.dma_start(
                out=xb().ap, in_=x_d[r0:r0 + 128, :]), (), [xb()])
            ssv = st(A, slice(cb0, cb0 + 1))
            P.op("act", lambda e: e.activation(out=xse().ap, in_=xb().ap, func=AF.Square,
                                               accum_out=ssv.ap), [xb()], [xse(), ssv])
            rs = st(A, slice(cb0 + 1, cb0 + 2))
            rstd_op(ssv, D, RMS_EPS, rs, cb0 + 2)
            P.op("act", lambda e: e.activation(out=xs().ap, in_=xb().ap, func=AF.Copy, scale=rs.ap),
                 [xb(), rs], [xs()])
            to_featmajor(xs, hT, 128 * m, g1T, m % 2)
        for m in range(5):
            _xtile(m)
        if stop_after == "hT" and ti == ntiles - 1:
            dump("hT", hT)
            stopped = True
            break

        first_tile = (ti == 0)
        P.op("dve", lambda e: e.tensor_copy(out=uT(A, A, slice(0, 32)).ap, in_=uprev().ap),
             [uprev()], [uT(A, A, slice(0, 32))])

        def conv_ops(c):
            a = acc[c % 2]
            wv = convw(A, c)
            bv = cvec(A, 0, slice(c, c + 1))
            ops = []
            ops.append((lambda e: e.tensor_scalar(out=a().ap, in0=uT.h[:, c, 1:513],
                                                  scalar1=convw.h[:, c, 0:1], scalar2=bv.ap,
                                                  op0=ALU.mult, op1=ALU.add),
                        [uT(A, c), wv, bv], [a()]))
            for k in range(1, 31):
                ops.append((lambda e, k=k: e.scalar_tensor_tensor(
                    out=a().ap, in0=uT.h[:, c, k + 1:k + 513], scalar=convw.h[:, c, k:k + 1],
                    in1=a().ap, op0=ALU.mult, op1=ALU.add), [uT(A, c), wv, a()], [a()]))
            ops.append((lambda e: e.tensor_copy(out=uprev(A, c).ap, in_=uT.h[:, c, 512:544]),
                        [uT(A, c)], [uprev(A, c)]))
            ops.append((lambda e: e.tensor_copy(out=uT.h[:, c, 0:512], in_=a().ap),
                        [a()], [uT(A, c)]))
            return ops

        dve_bg = []

        def dve_bg_flush(n=None):
            k = len(dve_bg) if n is None else min(n, len(dve_bg))
            for _ in range(k):
                dve_bg.pop(0)()

        def conv_pair(c0, c1, defer=False):
            o0, o1 = conv_ops(c0), conv_ops(c1)
            for x0, x1 in zip(o0, o1):
                for x in (x0, x1):
                    if defer:
                        dve_bg.append(lambda x=x: P.op("dve", *x))
                    else:
                        P.op("dve", *x)

        def glu_group(j, kind, dstb):
            p = next_set()
            xb_ = 4 * (1 - p)
            pieces = [(16, 512, 0)] + ([(0, 16, 1)] if first_tile else [])
            gemm_B(hT, pieces, 4,
                   lambda jj, pi, p=p, xb_=xb_: (4 * p + jj, 0) if pi == 0 else (xb_, 16 * jj))
            P.op("act", lambda e, p=p: e.activation(
                out=dstb.h[:, :, 16:528], in_=ps.h[:, 4 * p:4 * p + 4, :], func=kind),
                [ps.f(A, slice(4 * p, 4 * p + 4), 0, 512)], [dstb()])
            if first_tile:
                P.op("act", lambda e, xb_=xb_: e.activation(
                    out=dstb.h[:, :, 0:16], in_=ps.h[:, xb_, 0:64].rearrange("p (a b) -> p a b", a=4),
                    func=kind), [ps.f(A, xb_, 0, 64)], [dstb()])

        for j in range(4):
            sgb = sig[j % 2]
            cb = cab[j % 2]
            glu_group(j, AF.Sigmoid, sgb)
            glu_group(j, AF.Copy, cb)
            t0c = 0 if first_tile else 16
            uv = uT(A, slice(4 * j, 4 * j + 4), slice(16 + t0c, 544))
            P.op("dve", lambda e, uv=uv, cb=cb, sgb=sgb, t0c=t0c: e.tensor_tensor(
                out=uv.ap, in0=cb.h[:, :, t0c:528], in1=sgb.h[:, :, t0c:528], op=ALU.mult), [cb(), sgb()], [uv])
            conv_pair(4 * j, 4 * j + 1, defer=(j >= 2))
            conv_pair(4 * j + 2, 4 * j + 3, defer=(j >= 2))

        dve_bg_flush(64)
        def rope_batch(src_fn, c0, n, dst_view):
            cs = slice(c0, c0 + n)
            rv = rraw(slice(0, 32), A, cs)
            swv = rsw(slice(0, 32), A, cs)
            src_fn(rv)
            P.dma("sp", "rope", lambda e: e.dma_start(out=rsw.h[0:16, :, cs], in_=rraw.h[16:32, :, cs]),
                  [rv], [swv])
            P.dma("sp", "rope", lambda e: e.dma_start(out=rsw.h[16:32, :, cs], in_=rraw.h[0:16, :, cs]),
                  [rv], [swv])
            cv = Ct(slice(0, 32), cs)
            sv = St(slice(0, 32), cs)
            dve_bg_flush(NPRE)
            P.op(ROPE_ENG, lambda e: e.tensor_tensor(out=rv.ap, in0=rv.ap,
                                                     in1=cv.ap.unsqueeze(1).to_broadcast([32, 2, n]),
                                                     op=ALU.mult), [rv, cv], [rv])
            P.op(ROPE_ENG, lambda e: e.tensor_tensor(out=swv.ap, in0=swv.ap,
                                                     in1=sv.ap.unsqueeze(1).to_broadcast([32, 2, n]),
                                                     op=ALU.mult), [swv, sv], [swv])
            P.op(ROPE_ENG, lambda e: e.tensor_tensor(out=dst_view.ap, in0=rv.ap, in1=swv.ap, op=ALU.add),
                 [rv, swv], [dst_view])
            dve_bg_flush(NFLUSH - NPRE)

        p = next_set()
        xb_ = 4 * (1 - p)
        pieces = [(128, 512, 0)] + ([(0, 128, 1)] if first_tile else [])
        gemm_B(hT, pieces, 4,
               lambda jj, pi, p=p, xb_=xb_: (4 * p + jj, 0) if pi == 0 else (xb_, 128 * jj))
        for (pa, pb) in ((32, 64), (64, 128)):
            P.op("act", lambda e, p=p, pa=pa, pb=pb: e.activation(
                out=KT.h[pa:pb, :, 256:768], in_=ps.h[pa:pb, 4 * p:4 * p + 4, :], func=AF.Copy),
                [ps.f(A, slice(4 * p, 4 * p + 4), 0, 512)], [KT(A, A, slice(256, 768))])
            if first_tile:
                P.op("act", lambda e, xb_=xb_, pa=pa, pb=pb: e.activation(
                    out=KT.h[pa:pb, :, 128:256], in_=ps.h[pa:pb, xb_, :].rearrange("p (a b) -> p a b", a=4),
                    func=AF.Copy), [ps.f(A, xb_, 0, 512)], [KT(A, A, slice(128, 256))])
        for hb in range(2):
            def ksrc(rv, p=p, xb_=xb_, hb=hb):
                P.op("act", lambda e: e.activation(
                    out=rraw.h[0:32, :, 128:640], in_=ps.h[0:32, 4 * p + 2 * hb:4 * p + 2 * hb + 2, :], func=AF.Copy),
                    [ps.f(A, slice(4 * p + 2 * hb, 4 * p + 2 * hb + 2), 0, 512)], [rraw(A, A, slice(128, 640))])
                if first_tile:
                    P.op("act", lambda e: e.activation(
                        out=rraw.h[0:32, :, 0:128],
                        in_=ps.h[0:32, xb_, 256 * hb:256 * hb + 256].rearrange("p (a b) -> p a b", a=2),
                        func=AF.Copy), [ps.f(A, xb_, 256 * hb, 256 * hb + 256)], [rraw(A, A, slice(0, 128))])
            if first_tile:
                rope_batch(ksrc, 0, 640, View(KT.h[0:32, 2 * hb:2 * hb + 2, 128:768],
                                              KT(A, slice(2 * hb, 2 * hb + 2), slice(128, 768)).rg))
            else:
                rope_batch(ksrc, 128, 512, View(KT.h[0:32, 2 * hb:2 * hb + 2, 256:768],
                                                KT(A, slice(2 * hb, 2 * hb + 2), slice(256, 768)).rg))

        p = next_set()
        xb_ = 4 * (1 - p)
        vtiles = [(1, 4 * p), (2, 4 * p + 1), (3, 4 * p + 2), (4, 4 * p + 3)] + ([(0, xb_)] if first_tile else [])
        gemm_A(hT, vtiles, 4)
        P.op("act", lambda e, p=p: e.activation(out=VA.h[:, 2:6, :], in_=ps.h[:, 4 * p:4 * p + 4, :],
                                                func=AF.Copy),
             [ps.f(A, slice(4 * p, 4 * p + 4), 0, 512)], [VA(A, slice(2, 6))])
        if first_tile:
            P.op("act", lambda e, xb_=xb_: e.activation(out=VA.h[:, 1, :], in_=ps.h[:, xb_, :], func=AF.Copy),
                 [ps.f(A, xb_, 0, 512)], [VA(A, 1)])

        dve_bg_flush(NFLUSH)
        for j in range(4):
            p = next_set()
            gemm_B(hT, [(0, 512, 0)], 4, lambda jj, pi, p=p: (4 * p + jj, 0))
            for (pa, pb) in ((32, 64), (64, 128)):
                P.op("act", lambda e, p=p, pa=pa, pb=pb, j=j: e.activation(
                    out=QT.h[pa:pb, 4 * j:4 * j + 4, :], in_=ps.h[pa:pb, 4 * p:4 * p + 4, :], func=AF.Copy),
                    [ps.f(A, slice(4 * p, 4 * p + 4), 0, 512)], [QT(A, slice(4 * j, 4 * j + 4))])
            for hb in range(2):
                def qsrc(rv, p=p, hb=hb):
                    P.op("act", lambda e: e.activation(
                        out=rraw.h[0:32, :, 0:512], in_=ps.h[0:32, 4 * p + 2 * hb:4 * p + 2 * hb + 2, :],
                        func=AF.Copy),
                        [ps.f(A, slice(4 * p + 2 * hb, 4 * p + 2 * hb + 2), 0, 512)], [rraw(A, A, slice(0, 512))])
                h0 = 4 * j + 2 * hb
                rope_batch(qsrc, 0, 512, View(QT.h[0:32, h0:h0 + 2, :], QT(A, slice(h0, h0 + 2)).rg))
        if stop_after == "inproj" and ti == ntiles - 1:
            bg_flush()
            dve_bg_flush()
            dump("QT", QT)
            dump("KT", KT)
            dump("VA", VA)
            dump("uT", uT)
            stopped = True
            break

        bg_flush()
        dve_bg_flush()
        S1 = ps.f(A, 0, 0, 512)
        S2 = ps.f(A, 1, 0, 512)
        sqs = [lnt[0], acc[0]]
        for c in range(16):
            yv = uT(A, c, slice(0, 512))
            sq = sqs[c % 2]
            P.op("pe", lambda e, c=c: e.matmul(ps.h[:, 0, :], lhsT=ones.h[:, :], rhs=uT.h[:, c, 0:512],
                                               start=(c == 0), stop=(c == 15)), [ones(), yv], [S1])
            P.op("act", lambda e, c=c, sq=sq: e.activation(out=sq().ap, in_=uT.h[:, c, 0:512], func=AF.Square),
                 [yv], [sq()])
            P.op("pe", lambda e, c=c, sq=sq: e.matmul(ps.h[:, 1, :], lhsT=ones.h[:, :], rhs=sq.h[:, :],
                                                      start=(c == 0), stop=(c == 15)), [ones(), sq()], [S2])
        mean, msq, var, rstd_ln = lnt[1], lnt[2], lnt[3], lnt[4]
        P.op("act", lambda e: e.mul(out=mean().ap, in_=ps.h[:, 0, :], mul=1.0 / 2048),
             [S1], [mean()])
        P.op("dve", lambda e: e.tensor_tensor(out=msq().ap, in0=mean().ap, in1=mean().ap, op=ALU.mult),
             [mean()], [msq()])
        P.op("dve", lambda e: e.scalar_tensor_tensor(out=var().ap, in0=ps.h[:, 1, :], scalar=1.0 / 2048,
                                                     in1=msq().ap, op0=ALU.mult, op1=ALU.subtract),
             [S2, msq()], [var()])
        P.op("dve", lambda e: e.tensor_scalar(out=var().ap, in0=var().ap, scalar1=LN_EPS, scalar2=None,
                                              op0=ALU.add), [var()], [var()])
        P.op("act", lambda e: e.activation(out=var().ap, in_=var().ap, func=AF.Sqrt), [var()], [var()])
        P.op("dve", lambda e: e.reciprocal(out=rstd_ln().ap, in_=var().ap), [var()], [rstd_ln()])
        S3 = ps.f(A, 2, 0, 512)
        for c in range(16):
            yv = uT(A, c, slice(0, 512))
            P.op("dve", lambda e, c=c: e.tensor_tensor(out=uT.h[:, c, 0:512], in0=uT.h[:, c, 0:512],
                                                       in1=mean().ap, op=ALU.subtract), [yv, mean()], [yv])
            P.op("dve", lambda e, c=c: e.tensor_tensor(out=uT.h[:, c, 0:512], in0=uT.h[:, c, 0:512],
                                                       in1=rstd_ln().ap, op=ALU.mult), [yv, rstd_ln()], [yv])
            gv = cvec(A, 1, slice(c, c + 1))
            bv = cvec(A, 2, slice(c, c + 1))
            P.op("act", lambda e, c=c, gv=gv, bv=bv: e.activation(
                out=uT.h[:, c, 0:512], in_=uT.h[:, c, 0:512], func=AF.Silu, scale=gv.ap, bias=bv.ap),
                [yv, gv, bv], [yv])
            sq = sqs[c % 2]
            P.op("act", lambda e, c=c, sq=sq: e.activation(out=sq().ap, in_=uT.h[:, c, 0:512], func=AF.Square),
                 [yv], [sq()])
            P.op("pe", lambda e, c=c, sq=sq: e.matmul(ps.h[:, 2, :], lhsT=ones.h[:, :], rhs=sq.h[:, :],
                                                      start=(c == 0), stop=(c == 15)), [ones(), sq()], [S3])
        r3 = lnt[5]
        P.op("dve", lambda e: e.tensor_scalar(out=r3().ap, in0=ps.h[:, 2, :], scalar1=1.0 / 2048,
                                              scalar2=RMS_EPS, op0=ALU.mult, op1=ALU.add), [S3], [r3()])
        P.op("act", lambda e: e.activation(out=r3().ap, in_=r3().ap, func=AF.Sqrt), [r3()], [r3()])
        P.op("dve", lambda e: e.reciprocal(out=r3().ap, in_=r3().ap), [r3()], [r3()])
        for c in range(16):
            yv = uT(A, c, slice(0, 512))
            gv = cvec(A, 3, slice(c, c + 1))
            mv = mergedT(A, 16 + c)
            P.op("dve", lambda e, c=c, gv=gv, mv=mv: e.scalar_tensor_tensor(
                out=mv.ap, in0=uT.h[:, c, 0:512], scalar=gv.ap, in1=r3().ap, op0=ALU.mult, op1=ALU.mult),
                [yv, gv, r3()], [mv])

        P.dma("sp", "sa", lambda e: e.dma_start(out=attng().ap, in_=attn_g_d[0:1, :].partition_broadcast(128)),
              (), [attng()])
        groups = [(qb, kv) for qb in range(4) for kv in range(4)]

        def att_scores(gi):
            qb, kv = groups[gi]
            full = not (ti == 0 and qb == 0)
            kc0 = 128 * qb if full else 128
            nk = 384 if full else 256
            pc0 = 0 if full else 128
            for hh in range(4):
                h = 4 * kv + hh
                P.op("pe", lambda e, h=h, hh=hh: e.matmul(
                    ps.h[:, hh, pc0:pc0 + nk], lhsT=QT.h[:, h, 128 * qb:128 * qb + 128],
                    rhs=KT.h[:, kv, kc0:kc0 + nk], start=True, stop=True),
                    [QT(A, h, slice(128 * qb, 128 * qb + 128)), KT(A, kv, slice(kc0, kc0 + nk))],
                    [ps.f(A, hh, pc0, pc0 + nk)])

        def att_sm1(gi):
            qb, kv = groups[gi]
            full = not (ti == 0 and qb == 0)
            pc0 = 0 if full else 128
            nk = 384 - pc0
            b = gi % 2
            sb_ = s_sb[b]
            sv = sb_(A, A, slice(pc0, 384))
            P.op("dve", lambda e: e.scalar_tensor_tensor(
                out=sv.ap, in0=ps.h[:, 0:4, pc0:384], scalar=inv_sqrt_d,
                in1=maskb.h[:, pc0:384].unsqueeze(1).to_broadcast([128, 4, nk]),
                op0=ALU.mult, op1=ALU.add), [ps.f(A, slice(0, 4), pc0, 384), maskb()], [sv])
            mx = st2(A, b, slice(0, 4))
            P.op("dve", lambda e: e.tensor_reduce(out=mx.ap, in_=sv.ap, axis=AX.X, op=ALU.max), [sv], [mx])
            skv = sinkb(A, slice(4 * kv, 4 * kv + 4))
            P.op("dve", lambda e: e.tensor_tensor(out=mx.ap, in0=mx.ap, in1=skv.ap, op=ALU.max), [mx, skv], [mx])
            nmx = st2(A, b, slice(4, 8))
            P.op("dve", lambda e: e.tensor_scalar(out=nmx.ap, in0=mx.ap, scalar1=-1.0, scalar2=None,
                                                  op0=ALU.mult), [mx], [nmx])
            dsk = st2(A, b, slice(8, 12))
            P.op("dve", lambda e: e.tensor_tensor(out=dsk.ap, in0=skv.ap, in1=mx.ap, op=ALU.subtract),
                 [skv, mx], [dsk])

        def att_exps(gi):
            qb, kv = groups[gi]
            full = not (ti == 0 and qb == 0)
            pc0 = 0 if full else 128
            b = gi % 2
            sb_ = s_sb[b]
            pb_ = Pb[b]
            sv = sb_(A, A, slice(pc0, 384))
            nmx = st2(A, b, slice(4, 8))
            dsk = st2(A, b, slice(8, 12))
            rsum = st2(A, b, slice(12, 16))
            for hh in range(4):
                P.op("act", lambda e, hh=hh: e.activation(
                    out=pb_.h[:, hh, pc0:384], in_=sb_.h[:, hh, pc0:384], func=AF.Exp,
                    bias=st2.h[:, b, 4 + hh:5 + hh], accum_out=st2.h[:, b, 12 + hh:13 + hh]),
                    [sv, nmx], [pb_(A, hh), rsum])
            esk = st2(A, b, slice(16, 20))
            P.op("act", lambda e: e.activation(out=esk.ap, in_=dsk.ap, func=AF.Exp), [dsk], [esk])

        def att_sm2(gi):
            b = gi % 2
            rsum = st2(A, b, slice(12, 16))
            esk = st2(A, b, slice(16, 20))
            den = st2(A, b, slice(20, 24))
            P.op("dve", lambda e: e.tensor_tensor(out=den.ap, in0=rsum.ap, in1=esk.ap, op=ALU.add),
                 [rsum, esk], [den])
            rden = st2(A, b, slice(24, 28))
            P.op("dve", lambda e: e.reciprocal(out=rden.ap, in_=den.ap), [den], [rden])

        def att_pv(gi):
            qb, kv = groups[gi]
            full = not (ti == 0 and qb == 0)
            j0 = 0 if full else 1
            pb_ = Pb[gi % 2]
            pt_ = PTs[gi % 2]
            for half in range(2):
                bank = 4 + half
                pv_ = ps.b(A, bank, 0, 768)

                def tr(e, half=half, bank=bank):
                    ins = None
                    for hh2 in range(2):
                        hh = 2 * half + hh2
                        for j in range(j0, 3):
                            ins = e.transpose(ps.hb[:, bank, hh2 * 384 + j * 128: hh2 * 384 + (j + 1) * 128],
                                              pb_.h[:, hh, j * 128:(j + 1) * 128], ident.h[:, :])
                    return ins
                P.op("pe", tr, [pb_(A, slice(2 * half, 2 * half + 2)), ident()], [pv_])
                c_lo = 128 * j0
                P.op("act", lambda e, half=half, bank=bank, c_lo=c_lo: e.activation(
                    out=pt_.h[:, 2 * half:2 * half + 2, c_lo:384],
                    in_=ps.hb[:, bank, 0:768].rearrange("p (a b) -> p a b", a=2)[:, :, c_lo:384], func=AF.Copy),
                    [pv_], [pt_(A, slice(2 * half, 2 * half + 2))])
            ov = ps.f(A, 6, 0, 512)

            def pvmm(e):
                ins = None
                for hh in range(4):
                    for j in range(j0, 3):
                        ins = e.matmul(ps.h[:, 6, hh * 128:(hh + 1) * 128],
                                       lhsT=pt_.h[:, hh, j * 128:(j + 1) * 128],
                                       rhs=VA.h[:, qb + j, kv * 128:(kv + 1) * 128],
                                       start=(j == j0), stop=(j == 2))
                return ins
            P.op("pe", pvmm, [pt_(), VA(A, slice(qb, qb + 3))], [ov])
            rden = st2(A, gi % 2, slice(24, 28))
            av = atok(A, slice(512 * kv, 512 * kv + 512))
            P.op("dve", lambda e: e.tensor_tensor(
                out=av.ap.rearrange("p (a b) -> p a b", a=4),
                in0=ps.h[:, 6, :].rearrange("p (a b) -> p a b", a=4),
                in1=rden.ap.unsqueeze(2).to_broadcast([128, 4, 128]), op=ALU.mult), [ov, rden], [av])

        def att_finish(qb):
            ssv = st(A, slice(40, 41))
            P.op("act", lambda e: e.activation(out=anb().ap, in_=atok().ap, func=AF.Square, accum_out=ssv.ap),
                 [atok()], [anb(), ssv])
            rs = st(A, slice(41, 42))
            rstd_op(ssv, 2048, RMS_EPS, rs, 42)
            P.op("dve", lambda e: e.scalar_tensor_tensor(out=anb().ap, in0=atok().ap, scalar=rs.ap,
                                                         in1=attng().ap, op0=ALU.mult, op1=ALU.mult),
                 [atok(), rs, attng()], [anb()])
            for half in range(2):
                pv_ = ps.b(A, 7, 0, 1024)

                def tr(e, half=half):
                    ins = None
                    for i in range(8):
                        c = half * 8 + i
                        ins = e.transpose(ps.hb[:, 7, i * 128:(i + 1) * 128],
                                          anb.h[:, c * 128:(c + 1) * 128], ident.h[:, :])
                    return ins
                P.op("pe", tr, [anb(), ident()], [pv_])
                mv = mergedT(A, slice(half * 8, half * 8 + 8), slice(128 * qb, 128 * qb + 128))
                P.op("act", lambda e, mv=mv: e.activation(
                    out=mv.ap, in_=ps.hb[:, 7, :].rearrange("p (a b) -> p a b", a=8), func=AF.Copy),
                    [pv_], [mv])

        ng = len(groups)
        att_scores(0)
        att_sm1(0)
        for gi in range(ng):
            if gi + 1 < ng:
                att_scores(gi + 1)
            att_exps(gi)
            if gi + 1 < ng:
                att_sm1(gi + 1)
            att_sm2(gi)
            att_pv(gi)
            if groups[gi][1] == 3:
                att_finish(groups[gi][0])
        if ti + 1 < ntiles:
            P.op("act", lambda e: e.activation(out=KT.h[:, :, 0:256], in_=KT.h[:, :, 512:768], func=AF.Copy),
                 [KT(A, A, slice(512, 768))], [KT(A, A, slice(0, 256))])
            P.op("act", lambda e: e.activation(out=VA.h[:, 0:2, :], in_=VA.h[:, 4:6, :], func=AF.Copy),
                 [VA(A, slice(4, 6))], [VA(A, slice(0, 2))])
        if stop_after == "merged" and ti == ntiles - 1:
            dump("mergedT", mergedT)
            stopped = True
            break

        for j in range(8):
            p = next_set()
            gemm_A(mergedT, [(tt, 4 * p + tt) for tt in range(4)], 4)
            rv = r_sb(A, A, slice(512 * j, 512 * j + 512))
            P.op("act", lambda e, p=p, rv=rv: e.activation(out=rv.ap, in_=ps.h[:, 4 * p:4 * p + 4, :], func=AF.Copy),
                 [ps.f(A, slice(4 * p, 4 * p + 4), 0, 512)], [rv])
        P.dma("sp", "sb1", lambda e: e.dma_start(out=gbc1().ap, in_=post1_d[0:1, :].partition_broadcast(128)),
              (), [gbc1()])
        def _oepi(tt, s0=s0):
            r0 = s0 + 128 * tt
            b = tt % 2
            xeb = xe if b == 0 else xe2
            xsv = xsr[tt]
            c0 = 44 + 8 * b
            P.dma("sp", "xe%d" % b, lambda e: e.dma_start(out=xeb().ap, in_=x_d[r0:r0 + 128, :]), (), [xeb()])
            rv = r_sb(A, tt)
            ssv = st(A, slice(c0, c0 + 1))
            P.op("act", lambda e: e.activation(out=xse().ap, in_=rv.ap, func=AF.Square, accum_out=ssv.ap),
                 [rv], [xse(), ssv])
            rs = st(A, slice(c0 + 1, c0 + 2))
            rstd_op(ssv, D, RMS_EPS, rs, c0 + 2)
            P.op("dve", lambda e: e.scalar_tensor_tensor(out=rv.ap, in0=rv.ap, scalar=rs.ap, in1=gbc1().ap,
                                                         op0=ALU.mult, op1=ALU.mult), [rv, rs, gbc1()], [rv])
            P.op("dve", lambda e: e.tensor_tensor(out=xeb().ap, in0=xeb().ap, in1=rv.ap, op=ALU.add),
                 [xeb(), rv], [xeb()])
            cn = "ys%d" % b
            if cn not in final_chans:
                final_chans.append(cn)
            P.dma("sp", cn, lambda e: e.dma_start(out=y_d[r0:r0 + 128, :], in_=xeb().ap),
                  [xeb()], [DR(r0 // 128)])
            ss2 = st(A, slice(c0 + 4, c0 + 5))
            P.op("act", lambda e: e.activation(out=xse().ap, in_=xeb().ap, func=AF.Square, accum_out=ss2.ap),
                 [xeb()], [xse(), ss2])
            rs2 = st(A, slice(c0 + 5, c0 + 6))
            rstd_op(ss2, D, RMS_EPS, rs2, c0 + 6)
            P.op("act", lambda e: e.activation(out=xsv().ap, in_=xeb().ap, func=AF.Copy, scale=rs2.ap),
                 [xeb(), rs2], [xsv()])
            to_featmajor(xsv, h2T, 128 * tt, g2T, tt % 2)
        for tt in range(4):
            _oepi(tt)
        if stop_after == "h2T" and ti == ntiles - 1:
            dump("h2T", h2T)
            stopped = True
            break

        for j in range(22):
            nj = 4 if j < 21 else 2
            sg = sgsb[j % 2]
            p = next_set()
            gemm_B(h2T, [(0, 512, 0)], 4, lambda jj, pi, p=p: (4 * p + jj, 0))
            P.op("act", lambda e, p=p, sg=sg, nj=nj: e.activation(
                out=sg.h[:, 0:nj, :], in_=ps.h[:, 4 * p:4 * p + nj, :], func=AF.Silu),
                [ps.f(A, slice(4 * p, 4 * p + nj), 0, 512)], [sg(A, slice(0, nj))])
            p = next_set()
            gemm_B(h2T, [(0, 512, 0)], 4, lambda jj, pi, p=p: (4 * p + jj, 0))
            av = actT(A, slice(4 * j, 4 * j + nj))
            P.op("dve", lambda e, p=p, sg=sg, nj=nj, av=av: e.tensor_tensor(
                out=av.ap, in0=ps.h[:, 4 * p:4 * p + nj, :], in1=sg.h[:, 0:nj, :], op=ALU.mult),
                [ps.f(A, slice(4 * p, 4 * p + nj), 0, 512), sg(A, slice(0, nj))], [av])
        if stop_after == "act" and ti == ntiles - 1:
            dump("actT", actT)
            stopped = True
            break

        for j in range(8):
            p = next_set()
            gemm_A(actT, [(tt, 4 * p + tt) for tt in range(4)], 11)
            fv = f_sb(A, A, slice(512 * j, 512 * j + 512))
            P.op("act", lambda e, p=p, fv=fv: e.activation(out=fv.ap, in_=ps.h[:, 4 * p:4 * p + 4, :], func=AF.Copy),
                 [ps.f(A, slice(4 * p, 4 * p + 4), 0, 512)], [fv])
        P.dma("sp", "sb2", lambda e: e.dma_start(out=gbc2().ap, in_=post2_d[0:1, :].partition_broadcast(128)),
              (), [gbc2()])
        def _fepi(tt, s0=s0):
            r0 = s0 + 128 * tt
            xb = xf[tt % 2]
            P.dma("sp", "xf%d" % (tt % 2), lambda e, r0=r0, xb=xb: e.dma_start(out=xb().ap, in_=y_d[r0:r0 + 128, :]),
                  [DR(r0 // 128)], [xb()])
            fv = f_sb(A, tt)
            ssv = st(A, slice(4, 5))
            junk = actT(A, slice(80, 84))
            P.op("act", lambda e, fv=fv: e.activation(out=junk.ap.rearrange("p a b -> p (a b)"), in_=fv.ap[:, 0:2048],
                                                      func=AF.Square, accum_out=ssv.ap), [fv], [junk, ssv])
            ssv2 = st(A, slice(5, 6))
            P.op("act", lambda e, fv=fv: e.activation(out=junk.ap.rearrange("p a b -> p (a b)"), in_=fv.ap[:, 2048:4096],
                                                      func=AF.Square, accum_out=ssv2.ap), [fv], [junk, ssv2])
            sst = st(A, slice(6, 7))
            P.op("dve", lambda e: e.tensor_tensor(out=sst.ap, in0=ssv.ap, in1=ssv2.ap, op=ALU.add), [ssv, ssv2], [sst])
            rs = st(A, slice(7, 8))
            rstd_op(sst, D, RMS_EPS, rs, 8)
            P.op("dve", lambda e, fv=fv: e.scalar_tensor_tensor(out=fv.ap, in0=fv.ap, scalar=rs.ap, in1=gbc2().ap,
                                                                op0=ALU.mult, op1=ALU.mult), [fv, rs, gbc2()], [fv])
            P.op("dve", lambda e, fv=fv, xb=xb: e.tensor_tensor(out=xb().ap, in0=xb().ap, in1=fv.ap, op=ALU.add),
                 [xb(), fv], [xb()])
            cn = "yo%d" % (tt % 2)
            if cn not in final_chans:
                final_chans.append(cn)
            P.dma("sp", cn, lambda e, r0=r0, xb=xb: e.dma_start(out=y_d[r0:r0 + 128, :], in_=xb().ap),
                  [xb()], [DR(r0 // 128)])
        for tt in range(4):
            _fepi(tt)

    if "dbg" in P.chan or stopped:
        final_chans.append("dbg")
    P.emit(final_chans)
    return nc, dbg


def _consts():
    ident = np.eye(128, dtype=np.float32).astype(ml_dtypes.bfloat16)
    q = np.arange(128)[:, None]
    j = np.arange(384)[None, :]
    valid = np.where(j < 128, j >= q, np.where(j < 256, True, (j - 256) <= q))
    maskb = np.where(valid, 0.0, NEG).astype(np.float32)
    invf = (500000.0 ** (-np.arange(0, 32, 2, dtype=np.float32) / np.float32(32))).astype(np.float32)
    ropec = np.zeros((128, 4), np.float32)
    ropec[0:16, 0] = invf
    ropec[16:32, 0] = invf
    ropec[0:16, 1] = -1.0
    ropec[16:32, 1] = 1.0
    return ident, maskb, ropec


def _core_maps(inp):
    x = np.asarray(inp["x"], dtype=np.float32)
    pos = np.asarray(inp["positions"]).astype(np.int32)
    ident, maskb, ropec = _consts()

    def gT(v):
        return np.ascontiguousarray(np.asarray(v, np.float32).reshape(-1, 128).T)

    common = {
        "w_in": np.asarray(inp["w_in"], np.float32)[0],
        "w_out": np.asarray(inp["w_out"], np.float32)[0],
        "w_gate": np.asarray(inp["w_gate"], np.float32)[0],
        "w_up": np.asarray(inp["w_up"], np.float32)[0],
        "w_down": np.asarray(inp["w_down"], np.float32)[0],
        "g1T": gT(inp["mix_pre_g"][0]),
        "g2T": gT(inp["ffn_pre_g"][0]),
        "cvec": np.ascontiguousarray(np.concatenate(
            [gT(inp["conv_b"][0]), gT(inp["conv_ln_g"][0]), gT(inp["conv_ln_b"][0]), gT(inp["conv_out_g"][0])],
            axis=1)),
        "sinks": np.asarray(inp["sinks"], np.float32).reshape(1, 16),
        "attn_g": np.asarray(inp["attn_out_g"], np.float32).reshape(1, 2048),
        "post1": np.asarray(inp["mix_post_g"], np.float32).reshape(1, D),
        "post2": np.asarray(inp["ffn_post_g"], np.float32).reshape(1, D),
        "ident": ident, "maskb": maskb, "ropec": ropec,
    }
    cw = np.asarray(inp["conv_w"], np.float32)[0]
    def cwdev(c):
        return np.ascontiguousarray(c.T.reshape(16, 128, 31).transpose(1, 0, 2).reshape(128, 16 * 31))
    cw_f = cwdev(cw)
    cw_r = cwdev(cw[::-1])
    maps = []
    idxs = []
    for c in range(8):
        b, half = c // 2, c % 2
        if half == 0:
            idx = np.arange(0, NLOC)
        else:
            idx = 2047 - np.arange(0, NLOC)
        idxs.append((b, idx))
        m = dict(common)
        m["x"] = np.ascontiguousarray(x[b, idx])
        m["pos"] = np.ascontiguousarray(pos[b, idx]).reshape(1, NLOC)
        m["convw"] = cw_f if half == 0 else cw_r
        maps.append(m)
    return maps, idxs


_NC_CACHE = {}


def kernel(**inputs):
    maps, idxs = _core_maps(inputs)
    if "nc" not in _NC_CACHE:
        _NC_CACHE["nc"] = build_program()[0]
    nc = _NC_CACHE["nc"]
    res = run_bass_kernel_spmd(nc, maps, core_ids=list(range(8)))
    out = np.empty((4, 2048, D), np.float32)
    for c in range(8):
        b, idx = idxs[c]
        out[b, idx[:NOWN]] = np.asarray(res.results[c]["y"], np.float32)
    return out
```
